# Optimizing a Trainium2 kernel written in Bass

```python
import math
import jax, jax.numpy as jnp
from jax import lax
import numpy as np

D_MODEL = 1024
BATCH = 8
SEQ = 2048
DEPTH = 4

GRID_W = 64
CTX_LEN = 256
N_MIXERS = 3
N_NA = len(range(0, DEPTH, N_MIXERS))
N_LRU = len(range(1, DEPTH, N_MIXERS))
N_SC = len(range(2, DEPTH, N_MIXERS))
NA_HEADS = 16
NA_HEAD_DIM = D_MODEL // NA_HEADS
WIN_ROWS = 8
WIN_COLS = 16
D_RNN = 128 * ((4 * D_MODEL // 3 + 64) // 128)
LRU_BLOCKS = 16
LRU_BLOCK_W = D_RNN // LRU_BLOCKS
LRU_CONV_W = 4
LRU_C = 8.0
SC_CONV_W = 3
D_FF = 256 * ((8 * D_MODEL // 3 + 128) // 256)
N_MOD = 9
ALPHA = (2 * DEPTH) ** 0.25
BETA = (8 * DEPTH) ** -0.25
LN_EPS = 1e-5
NEG_INF = -1e30

kernel_name = "hybrid_na_rglru_shortconv_prefix_dit"


def layer_norm(x, g, b):
    xf = x.astype(jnp.float32)
    mu = xf.mean(-1, keepdims=True)
    var = jnp.square(xf - mu).mean(-1, keepdims=True)
    return ((xf - mu) * lax.rsqrt(var + LN_EPS)).astype(x.dtype) * g + b


def modulation(cond, w, b):
    m = jax.nn.silu(cond) @ w + b
    return m.reshape(cond.shape[0], N_MOD, 1, D_MODEL)


def modulate(t, m, j):
    return t * (1 + m[:, 3 * j + 1]) + m[:, 3 * j]


def post_norm(t, m, j, y, g, b):
    return layer_norm(ALPHA * t + m[:, 3 * j + 2] * y, g, b)


def swiglu(h, w_in, w_out):
    g, u = jnp.split(h @ w_in, 2, axis=-1)
    return (jax.nn.silu(g) * u) @ w_out


def ffn_half(t, m, j, w_in, w_out, g, b):
    return post_norm(t, m, j, 0.5 * swiglu(modulate(t, m, j), w_in, w_out), g, b)


def dwconv_centred(u, w):
    k = w.shape[0]
    left = k // 2
    right = k - 1 - left
    return lax.conv_general_dilated(
        u, w[:, None, :].astype(u.dtype), window_strides=(1,), padding=[(left, right)],
        dimension_numbers=('NWC', 'WIO', 'NWC'), feature_group_count=u.shape[-1])


def na_mixer(h, hc, w_qkv, w_o, rpb, need_ctx):
    B, S, _ = h.shape
    L = hc.shape[1]
    rows = S // GRID_W
    kr = min(WIN_ROWS, rows)
    scale = NA_HEAD_DIM ** -0.5
    grid = (B, rows, GRID_W, NA_HEADS, NA_HEAD_DIM)
    q, k, v = [t.reshape(grid) for t in jnp.split(h @ w_qkv, 3, axis=-1)]
    kc, vc = [t.reshape(B, L, NA_HEADS, NA_HEAD_DIM) for t in jnp.split(hc @ w_qkv[:, D_MODEL:], 2, axis=-1)]

    col = jnp.arange(GRID_W)
    col_start = jnp.clip(col - WIN_COLS // 2, 0, GRID_W - WIN_COLS)
    kcol = col[None, :]
    col_in = (kcol >= col_start[:, None]) & (kcol < col_start[:, None] + WIN_COLS)
    dcol_idx = jnp.clip(kcol - col[:, None], 1 - WIN_COLS, WIN_COLS - 1) + WIN_COLS - 1

    def row_block(r):
        r0 = jnp.clip(r - kr // 2, 0, rows - kr)
        kb = lax.dynamic_slice_in_dim(k, r0, kr, axis=1)
        vb = lax.dynamic_slice_in_dim(v, r0, kr, axis=1)
        qr = lax.dynamic_index_in_dim(q, r, axis=1, keepdims=False)
        drow_idx = r0 + jnp.arange(kr) - r + WIN_ROWS - 1
        bias = rpb[:, drow_idx[None, :, None], dcol_idx[:, None, :]].astype(jnp.float32)
        s_loc = jnp.einsum('bqhd,bjkhd->bhqjk', qr, kb).astype(jnp.float32) * scale + bias
        s_loc = jnp.where(col_in[:, None, :], s_loc, NEG_INF)
        s_ctx = jnp.einsum('bqhd,blhd->bhql', qr, kc).astype(jnp.float32) * scale
        s = jnp.concatenate([s_loc.reshape(B, NA_HEADS, GRID_W, kr * GRID_W), s_ctx], axis=-1)
        p = jax.nn.softmax(s, axis=-1).astype(vb.dtype)
        p_loc = p[..., :kr * GRID_W].reshape(B, NA_HEADS, GRID_W, kr, GRID_W)
        p_ctx = p[..., kr * GRID_W:]
        return (jnp.einsum('bhqjk,bjkhd->bqhd', p_loc, vb)
                + jnp.einsum('bhql,blhd->bqhd', p_ctx, vc))

    o = lax.map(row_block, jnp.arange(rows))
    y = jnp.moveaxis(o, 0, 1).reshape(B, S, D_MODEL) @ w_o
    yc = None
    if need_ctx:
        qc = (hc @ w_qkv[:, :D_MODEL]).reshape(B, L, NA_HEADS, NA_HEAD_DIM)
        sc = jnp.einsum('bqhd,bkhd->bhqk', qc, kc).astype(jnp.float32) * scale
        pc = jax.nn.softmax(sc, axis=-1).astype(vc.dtype)
        yc = jnp.einsum('bhqk,bkhd->bqhd', pc, vc).reshape(B, L, D_MODEL) @ w_o
    return y, yc


def rglru_coeffs(u, w_g, b_g, lam):
    B, T, _ = u.shape
    ub = u.reshape(B, T, LRU_BLOCKS, LRU_BLOCK_W)
    gates = jnp.einsum('btnk,gnkj->gbtnj', ub, w_g).reshape(2, B, T, D_RNN) + b_g[:, None, None]
    gates = gates.astype(jnp.float32)
    r = jax.nn.sigmoid(gates[0])
    i = jax.nn.sigmoid(gates[1])
    log_a = -LRU_C * r * jax.nn.softplus(-lam.astype(jnp.float32))
    a = jnp.exp(log_a)
    b = jnp.sqrt(-jnp.expm1(2 * log_a)) * (i * u.astype(jnp.float32))
    return a, b


def linear_scan(a, b, reverse):
    def combine(e1, e2):
        a1, b1 = e1
        a2, b2 = e2
        return a1 * a2, a2 * b1 + b2
    return lax.associative_scan(combine, (a, b), axis=1, reverse=reverse)


def lru_mixer(h, hc, w_in, conv_w, conv_b, w_g, b_g, lam, w_out, need_ctx):
    gate_l, u_l = jnp.split(h @ w_in, 2, axis=-1)
    u_l = dwconv_centred(u_l, conv_w) + conv_b
    if need_ctx:
        gate_c, u_c = jnp.split(hc @ w_in, 2, axis=-1)
    else:
        u_c = hc @ w_in[:, D_RNN:]
    u_c = dwconv_centred(u_c, conv_w) + conv_b
    ys_l = []
    ys_c = []
    for d, rev in enumerate((False, True)):
        a_c, b_c = rglru_coeffs(u_c, w_g[d], b_g[d], lam[d])
        _, h_c = linear_scan(a_c, b_c, rev)
        h0 = h_c[:, 0] if rev else h_c[:, -1]
        a_l, b_l = rglru_coeffs(u_l, w_g[d], b_g[d], lam[d])
        a_cum, h_l = linear_scan(a_l, b_l, rev)
        ys_l.append(h_l + a_cum * h0[:, None])
        if need_ctx:
            ys_c.append(h_c)
    y = (jax.nn.gelu(gate_l) * (ys_l[0] + ys_l[1]).astype(h.dtype)) @ w_out
    yc = None
    if need_ctx:
        yc = (jax.nn.gelu(gate_c) * (ys_c[0] + ys_c[1]).astype(hc.dtype)) @ w_out
    return y, yc


def sc_mixer(h, hc, w_in, conv_w, w_out, need_ctx):
    def one(t):
        bg, cg, u = jnp.split(t @ w_in, 3, axis=-1)
        return (bg * dwconv_centred(cg * u, conv_w)) @ w_out
    return one(h), (one(hc) if need_ctx else None)


def setup_inputs(seed: int = 0) -> dict:
    key = jax.random.key(seed)
    ks = jax.random.split(key, 24)
    f32 = jnp.float32

    def nrm(k, shape, scale):
        return jax.random.normal(k, shape, f32) * scale

    a_pow = jax.random.uniform(ks[18], (N_LRU, 2, D_RNN), f32, 0.9, 0.999)
    a_base = a_pow ** (1.0 / LRU_C)
    return {
        "x": nrm(ks[0], (BATCH, SEQ, D_MODEL), 1.0),
        "c": nrm(ks[1], (BATCH, D_MODEL), 1.0),
        "ctx": nrm(ks[2], (BATCH, CTX_LEN, D_MODEL), 1.0),
        "c_ctx": nrm(ks[3], (D_MODEL,), 1.0),
        "mod_w": nrm(ks[4], (DEPTH, D_MODEL, N_MOD * D_MODEL), 0.5 * D_MODEL ** -0.5),
        "mod_b": nrm(ks[5], (DEPTH, N_MOD * D_MODEL), 0.02),
        "ln_g": 1.0 + nrm(ks[6], (DEPTH, 3, D_MODEL), 0.02),
        "ln_b": nrm(ks[7], (DEPTH, 3, D_MODEL), 0.02),
        "ffn_w_in": nrm(ks[8], (DEPTH, 2, D_MODEL, 2 * D_FF), D_MODEL ** -0.5),
        "ffn_w_out": nrm(ks[9], (DEPTH, 2, D_FF, D_MODEL), BETA * D_FF ** -0.5),
        "na_w_qkv": nrm(ks[10], (N_NA, D_MODEL, 3 * D_MODEL), D_MODEL ** -0.5),
        "na_w_o": nrm(ks[11], (N_NA, D_MODEL, D_MODEL), BETA * D_MODEL ** -0.5),
        "na_rpb": nrm(ks[12], (N_NA, NA_HEADS, 2 * WIN_ROWS - 1, 2 * WIN_COLS - 1), 0.1),
        "lru_w_in": nrm(ks[13], (N_LRU, D_MODEL, 2 * D_RNN), D_MODEL ** -0.5),
        "lru_conv_w": nrm(ks[14], (N_LRU, LRU_CONV_W, D_RNN), LRU_CONV_W ** -0.5),
        "lru_conv_b": nrm(ks[15], (N_LRU, D_RNN), 0.02),
        "lru_w_gates": nrm(ks[16], (N_LRU, 2, 2, LRU_BLOCKS, LRU_BLOCK_W, LRU_BLOCK_W), LRU_BLOCK_W ** -0.5),
        "lru_b_gates": nrm(ks[17], (N_LRU, 2, 2, D_RNN), 0.02),
        "lru_lambda": jnp.log(a_base) - jnp.log1p(-a_base),
        "lru_w_out": nrm(ks[19], (N_LRU, D_RNN, D_MODEL), BETA * D_RNN ** -0.5),
        "sc_w_in": nrm(ks[20], (N_SC, D_MODEL, 3 * D_MODEL), D_MODEL ** -0.5),
        "sc_conv_w": nrm(ks[21], (N_SC, SC_CONV_W, D_MODEL), SC_CONV_W ** -0.5),
        "sc_w_out": nrm(ks[22], (N_SC, D_MODEL, D_MODEL), BETA * D_MODEL ** -0.5),
    }


def reference(x, c, ctx, c_ctx, mod_w, mod_b, ln_g, ln_b, ffn_w_in, ffn_w_out, na_w_qkv, na_w_o, na_rpb,
              lru_w_in, lru_conv_w, lru_conv_b, lru_w_gates, lru_b_gates, lru_lambda, lru_w_out,
              sc_w_in, sc_conv_w, sc_w_out):
    xc = ctx
    for l in range(DEPTH):
        kind = l % N_MIXERS
        idx = l // N_MIXERS
        ctx_out = l < DEPTH - 1
        ctx_in = ctx_out or kind != 2
        m = modulation(c, mod_w[l], mod_b[l])
        mc = modulation(c_ctx[None], mod_w[l], mod_b[l]) if ctx_in else None

        x = ffn_half(x, m, 0, ffn_w_in[l, 0], ffn_w_out[l, 0], ln_g[l, 0], ln_b[l, 0])
        if ctx_in:
            xc = ffn_half(xc, mc, 0, ffn_w_in[l, 0], ffn_w_out[l, 0], ln_g[l, 0], ln_b[l, 0])

        h = modulate(x, m, 1)
        hc = modulate(xc, mc, 1) if ctx_in else None
        if kind == 0:
            y, yc = na_mixer(h, hc, na_w_qkv[idx], na_w_o[idx], na_rpb[idx], ctx_out)
        elif kind == 1:
            y, yc = lru_mixer(h, hc, lru_w_in[idx], lru_conv_w[idx], lru_conv_b[idx], lru_w_gates[idx],
                              lru_b_gates[idx], lru_lambda[idx], lru_w_out[idx], ctx_out)
        else:
            y, yc = sc_mixer(h, hc, sc_w_in[idx], sc_conv_w[idx], sc_w_out[idx], ctx_out)
        x = post_norm(x, m, 1, y, ln_g[l, 1], ln_b[l, 1])

        x = ffn_half(x, m, 2, ffn_w_in[l, 1], ffn_w_out[l, 1], ln_g[l, 2], ln_b[l, 2])
        if ctx_out:
            xc = post_norm(xc, mc, 1, yc, ln_g[l, 1], ln_b[l, 1])
            xc = ffn_half(xc, mc, 2, ffn_w_in[l, 1], ffn_w_out[l, 1], ln_g[l, 2], ln_b[l, 2])
    return x
```

```python
import numpy as np
from contextlib import ExitStack
import concourse.bass as bass
import concourse.mybir as mybir
from concourse.bass_utils import run_bass_kernel_spmd

F32 = mybir.dt.float32
BF16 = mybir.dt.bfloat16
AF = mybir.ActivationFunctionType
ALU = mybir.AluOpType

D = 1024
NCH = 8
TL = 2048
TC = 256
NT = TL + TC
DEPTH = 4
DFF = 2816
NJ = 22
NJH = 11
DRNN = 1408
NBLK = 16
BW = 88
ALPHA = (2 * DEPTH) ** 0.25
LN_EPS = 1e-5
EPS_P = LN_EPS / (ALPHA * ALPHA)
HEADS = 16
HD = 64
GRID_W = 64
ROWS = 32
WIN_R = 8
WIN_C = 16

ENGS = ("pe", "act", "dve", "pool", "sp")
N_DMA_SEMS = 24
N_SW_SEMS = 16
RING_SLOTS = 6
SLOT_ELEMS = 2048

TILES = [(0, 0, 512, 0), (1, 512, 512, 0), (2, 1024, 512, 0), (3, 1536, 512, 0), (4, 2048, 256, 1)]


class K:
    def __init__(self, nc, stack):
        self.nc = nc
        self.eng = {"pe": nc.tensor, "act": nc.scalar, "dve": nc.vector, "pool": nc.gpsimd, "sp": nc.sync}
        self.sem = {e: stack.enter_context(nc.semaphore("prog_" + e)) for e in ENGS}
        self.tick = {e: 0 for e in ENGS}
        self.dsem = [stack.enter_context(nc.semaphore("dma_%d" % i)) for i in range(N_DMA_SEMS)]
        self.dtick = [0] * N_DMA_SEMS
        self.dnext_sw = 0
        self.dnext_hw = 0
        self.seen = {e: {} for e in ENGS}
        self.clocks = {}
        self.last_w = {}
        self.readers = {}
        self.n_wait = 0
        self.n_ops = 0

    def _semof(self, src):
        return self.sem[src] if isinstance(src, str) else self.dsem[src]

    def _need(self, reads, writes):
        need = {}

        def add(src, t):
            if t > need.get(src, 0):
                need[src] = t

        for k in reads:
            lw = self.last_w.get(k)
            if lw:
                add(*lw)
        for k in writes:
            lw = self.last_w.get(k)
            if lw:
                add(*lw)
            rd = self.readers.get(k)
            if rd:
                for src, t in rd.items():
                    add(src, t)
        return need

    def _wait(self, e, need):
        seen = self.seen[e]
        for src, t in sorted(need.items(), key=lambda kv: str(kv[0])):
            if seen.get(src, 0) >= t:
                continue
            self.eng[e].wait_ge(self._semof(src), t)
            self.n_wait += 1
            seen[src] = t
            snap = self.clocks.get((src, t))
            if snap:
                for s2, t2 in snap.items():
                    if seen.get(s2, 0) < t2:
                        seen[s2] = t2

    def _record(self, src, t, reads, writes):
        for k in reads:
            self.readers.setdefault(k, {})[src] = t
        for k in writes:
            self.last_w[k] = (src, t)
            self.readers[k] = {}

    def op(self, e, fn, reads=(), writes=()):
        self._wait(e, self._need(reads, writes))
        ins = fn(self.eng[e])
        self.tick[e] += 1
        t = self.tick[e]
        ins.then_inc(self.sem[e], 1)
        self.clocks[(e, t)] = dict(self.seen[e])
        self._record(e, t, reads, writes)
        self.n_ops += 1
        return ins

    def dma(self, q, out, in_, reads=(), writes=(), **kw):
        if q == "pool":
            i = self.dnext_sw
            self.dnext_sw = (self.dnext_sw + 1) % N_SW_SEMS
        else:
            i = N_SW_SEMS + self.dnext_hw
            self.dnext_hw = (self.dnext_hw + 1) % (N_DMA_SEMS - N_SW_SEMS)
        need = self._need(reads, writes)
        if self.dtick[i] > 0 and need.get(i, 0) < self.dtick[i]:
            need[i] = self.dtick[i]
        self._wait(q, need)
        ins = self.eng[q].dma_start(out=out, in_=in_, **kw)
        self.dtick[i] += 16
        t = self.dtick[i]
        ins.then_inc(self.dsem[i], 16)
        self.clocks[(i, t)] = dict(self.seen[q])
        self._record(i, t, reads, writes)
        return ins

    def wait_all(self, e):
        need = {s: self.tick[s] for s in ENGS if self.tick[s] > 0}
        for i in range(N_DMA_SEMS):
            if self.dtick[i] > 0:
                need[i] = self.dtick[i]
        self._wait(e, need)


def bcast_mid(ap2d, n):
    return ap2d.unsqueeze(1).broadcast_to([ap2d.shape[0], n, ap2d.shape[1]])


class Prog:
    def __init__(self, stop_after=None, start_layer=0):
        self.stop_after = stop_after
        self.start_layer = start_layer
        self.na_norm_eng = "dve"
        self.nc = bass.Bass("TRN2", target_bir_lowering=False)
        nc = self.nc
        dt = nc.dram_tensor
        self.d = {}

        def din(name, shape):
            self.d[name] = dt(name, list(shape), F32, kind="ExternalInput").ap()

        din("xin", [D, NT])
        din("cc", [128, NCH, 2])
        din("modw", [DEPTH, 36, 128, 8 * 256])
        din("modb", [128, DEPTH, 72])
        din("lng", [128, DEPTH, 3, NCH])
        din("lnb", [128, DEPTH, 3, NCH])
        din("ffw1", [DEPTH, 2, NJ, 128, 8 * 256])
        din("ffw2", [DEPTH, 2, 2, NCH, 128, NJH * 128])
        din("ident", [128, 128])
        din("scw1", [NCH, 3, 128, 8 * 128])
        din("scw2", [NCH, 128, 8 * 128])
        din("sccw", [128, 3, NCH])
        din("lruw1", [NBLK, 128, 8 * 2 * BW])
        din("lrug", [NBLK, BW, 4 * BW])
        din("lruw2", [NBLK, BW, D])
        din("lruvec", [BW, NBLK, 11])
        din("naw1", [2, NCH, 3, 128, 8 * 128])
        din("naw2", [2, NCH, 128, 8 * 128])
        din("rpbt", [2, HEADS, 64, 15 * 64])
        din("cmask", [128, 64])
        self.out = dt("out", [D, NT], F32, kind="ExternalOutput").ap()

    def build(self):
        nc = self.nc
        with ExitStack() as st:
            self.st = st
            self.k = K(nc, st)
            sb = lambda name, shape, dtype: st.enter_context(nc.sbuf_tensor(name, list(shape), dtype))
            self.X = sb("X", [128, NCH, NT], F32)
            self.H = sb("H", [128, NCH, NT], BF16)
            self.A = sb("A", [128, NJH * NT], BF16)
            self.WR = sb("WR", [128, RING_SLOTS, SLOT_ELEMS], BF16)
            self.SCR = sb("SCR", [128, 2304], F32)
            self.LNT = sb("LNT", [128, 4 * 512], F32)
            self.T1 = self.LNT[:, 0:512]
            self.T2 = self.LNT[:, 512:1024]
            self.RS = self.LNT[:, 1024:1536]
            self.NM = self.LNT[:, 1536:2048]
            self.STAT = sb("STAT", [128, 32], F32)
            self.CMASK = sb("CMASK", [128, 64], F32)
            self.LV = sb("LV", [BW, NBLK, 24], F32)
            self.M = sb("M", [128, 72, 2], F32)
            self.VEC = sb("VEC", [128, 36, NCH, 2], F32)
            self.LNG = sb("LNG", [128, DEPTH, 3, NCH], F32)
            self.LNB = sb("LNB", [128, DEPTH, 3, NCH], F32)
            self.MODB = sb("MODB", [128, DEPTH, 72], F32)
            self.CC = sb("CC", [128, NCH, 2], F32)
            self.SC16 = sb("SC16", [128, NCH, 2], BF16)
            self.IDF = self.LNT[:, 0:128]
            self.IDB = sb("IDB", [128, 128], BF16)
            self.ONESD = sb("ONESD", [128, 128], BF16)
            self.S1N = sb("S1N", [128, NCH, 2], F32)
            self.EPS = sb("EPS", [128, 1], F32)
            self.CONST = sb("CONST", [128, 4], F32)
            self.PS2 = [st.enter_context(nc.psum_tensor("ps%d" % i, [128, 1024], F32)) for i in range(4)]
            self.PS = [self.PS2[i // 2][:, (i % 2) * 512:(i % 2 + 1) * 512] for i in range(8)]
            self.bg = []
            self.ring_limit = RING_SLOTS
            self.ring_next = 0
            self.vec_next = 0
            self.vidx = {}
            self._body()
            self.k.wait_all("sp")
        return nc

    def wload(self, dram_ap, n_elems, parts=128):
        assert n_elems <= SLOT_ELEMS
        s = self.ring_next
        self.ring_next = (s + 1) % self.ring_limit
        dst = self.WR[0:parts, s, 0:n_elems]
        self.k.dma("pool", dst, dram_ap, writes=[("W", s)])
        return s, dst

    def bg_step(self, n=1):
        for _ in range(n):
            if self.bg:
                self.bg.pop(0)()

    def bg_flush(self):
        while self.bg:
            self.bg.pop(0)()

    def valloc(self, name):
        i = self.vec_next
        self.vec_next += 1
        self.vidx[name] = i
        return i

    def V(self, name, c, s):
        return self.VEC[:, self.vidx[name], c, s:s + 1]

    def Vfull(self, name):
        return self.VEC[:, self.vidx[name], :, :]

    def atail(self):
        base = NJH * NT - 2 * NCH * 512
        xb = self.A[:, base:base + NCH * 512].rearrange("p (c t) -> p c t", c=NCH)
        xq = self.A[:, base + NCH * 512:base + 2 * NCH * 512].rearrange("p (c t) -> p c t", c=NCH)
        keys = [("A", jj, ti) for jj in range(7, NJH) for ti in range(5)]
        return xb, xq, keys

    def Aview(self, jj):
        return self.A[:, jj * NT:(jj + 1) * NT]

    def _body(self):
        k = self.k
        d = self.d
        k.dma("sp", self.CC[:], d["cc"], writes=["CC"])
        k.dma("sp", self.MODB[:], d["modb"], writes=["MODB"])
        k.dma("sp", self.LNG[:], d["lng"], writes=["LNG"])
        k.dma("sp", self.LNB[:], d["lnb"], writes=["LNB"])
        k.dma("sp", self.IDF, d["ident"], writes=["T1"])
        k.op("dve", lambda e: e.tensor_copy(out=self.IDB[:], in_=self.IDF), reads=["T1"], writes=["IDB"])
        k.op("dve", lambda e: e.memset(self.ONESD[:], 1.0 / D), writes=["ONESD"])
        k.op("dve", lambda e: e.memset(self.EPS[:], EPS_P), writes=["EPS"])
        k.op("dve", lambda e: e.memset(self.CONST[:, 0:1], 1.0), writes=["CONST"])
        k.op("dve", lambda e: e.memset(self.CONST[:, 1:2], 0.25), writes=["CONST"])
        k.op("dve", lambda e: e.memset(self.CONST[:, 2:3], 0.0), writes=["CONST"])
        xv = d["xin"].rearrange("(c p) t -> p c t", p=128)
        for (ti, t0, tn, s) in TILES:
            k.dma("sp", self.X[:, :, t0:t0 + tn], xv[:, :, t0:t0 + tn], writes=[("X", c, ti) for c in range(NCH)])
        k.op("act", lambda e: e.activation(out=self.SC16[:], in_=self.CC[:], func=AF.Silu), reads=["CC"], writes=["SC16"])
        first_jobs = self.mod_jobs(self.start_layer)
        for jb in first_jobs[:12]:
            jb()
        self.mod_vectors(self.start_layer, part=0)
        self.bg.extend(first_jobs[12:])
        self._first_mod_pending = True
        for (ti, t0, tn, s) in TILES:
            for c in range(NCH):
                k.op("dve", lambda e, c=c, t0=t0, tn=tn, s=s: e.tensor_scalar(
                    out=self.H[:, c, t0:t0 + tn], in0=self.X[:, c, t0:t0 + tn],
                    scalar1=self.V("HS_init", c, s), scalar2=self.V("HB_init", c, s), op0=ALU.mult, op1=ALU.add),
                    reads=[("X", c, ti), "VEC"], writes=[("H", ti)])
        done = False
        for l in range(self.start_layer, DEPTH):
            last = (l == DEPTH - 1)
            if not last and l != self.start_layer:
                self.bg.extend(self.mod_jobs(l + 1))
            for j in range(3):
                tiles = TILES if not (last and j >= 1) else TILES[:4]
                if j == 1:
                    self.mixer(l, tiles)
                else:
                    self.ffn(l, j // 2, j, tiles)
                if j == 0 and l == self.start_layer:
                    self.bg_flush()
                    self.mod_vectors(l, part=1)
                elif j == 0 and not last:
                    self.bg_flush()
                    self.mod_vectors(l + 1)
                if j == 1 and l == self.start_layer and not last:
                    self.bg.extend(self.mod_jobs(l + 1))
                if j == 2 and l == self.start_layer and not last:
                    self.bg_flush()
                    self.mod_vectors(l + 1)
                self.post_norm(l, j, tiles, make_h=not (last and j == 2))
                if self.stop_after == (l, j):
                    done = True
                    break
            if done:
                break
        ov = self.out.rearrange("(c p) t -> p c t", p=128)
        for (ti, t0, tn, s) in TILES:
            k.dma("sp", ov[:, :, t0:t0 + tn], self.X[:, :, t0:t0 + tn],
                  reads=[("X", c, ti) for c in range(NCH)], writes=[("OUT", ti)])

    def mod_jobs(self, l):
        k = self.k
        psm = self.PS[6]

        def job(s36):
            def run():
                slot, w = self.wload(self.d["modw"][l, s36], 8 * 256)
                wv = w.rearrange("p (kc n) -> p kc n", kc=8)

                def mm(e):
                    for q2 in range(2):
                        for kc in range(8):
                            ins = e.matmul(psm[:, 2 * q2:2 * q2 + 2], lhsT=wv[:, kc, q2 * 128:(q2 + 1) * 128],
                                           rhs=self.SC16[:, kc, :], start=(kc == 0), stop=(kc == 7))
                    return ins
                k.op("pe", mm, reads=[("W", slot), "SC16"], writes=[("PS", 6)])
                mb = self.MODB[:, l, 2 * s36:2 * s36 + 2].unsqueeze(2).broadcast_to([128, 2, 2])
                k.op("dve", lambda e: e.tensor_tensor(out=self.M[:, 2 * s36:2 * s36 + 2, :], in0=psm[:, 0:4].rearrange("p (q s) -> p q s", s=2),
                                                      in1=mb, op=ALU.add),
                     reads=["MODB"], writes=[("PS", 6), "M"])
            return run
        return [job(i) for i in range(36)]

    def mod_vectors(self, l, part=None):
        k = self.k

        def msl(jm):
            return self.M[:, jm * 8:(jm + 1) * 8, :]

        def hs_hb(name_l, name_j, nj):
            ihs = self.valloc(("HS", name_l, name_j))
            ihb = self.valloc(("HB", name_l, name_j))
            k.op("dve", lambda e: e.tensor_scalar(out=self.S1N[:], in0=msl(3 * nj + 1), scalar1=1.0, scalar2=None, op0=ALU.add),
                 reads=["M"], writes=["S1N"])
            lg = self.LNG[:, name_l, name_j, :].unsqueeze(2).broadcast_to([128, NCH, 2])
            lb = self.LNB[:, name_l, name_j, :].unsqueeze(2).broadcast_to([128, NCH, 2])
            k.op("dve", lambda e: e.tensor_tensor(out=self.VEC[:, ihs], in0=self.S1N[:], in1=lg, op=ALU.mult),
                 reads=["S1N", "LNG"], writes=["VEC"])
            k.op("dve", lambda e: e.tensor_tensor(out=self.VEC[:, ihb], in0=self.S1N[:], in1=lb, op=ALU.mult),
                 reads=["S1N", "LNB"], writes=["VEC"])
            k.op("dve", lambda e: e.tensor_tensor(out=self.VEC[:, ihb], in0=self.VEC[:, ihb], in1=msl(3 * nj), op=ALU.add),
                 reads=["VEC", "M"], writes=["VEC"])
        if part == 1:
            for j in (1, 2):
                fac = (0.5 if j != 1 else 1.0) / ALPHA
                ig = self.valloc(("GP", l, j))
                k.op("dve", lambda e: e.tensor_scalar(out=self.VEC[:, ig], in0=msl(3 * j + 2), scalar1=fac, scalar2=None, op0=ALU.mult),
                     reads=["M"], writes=["VEC"])
            hs_hb(l, 0, 1)
            hs_hb(l, 1, 2)
            return
        if l == self.start_layer:
            i0 = self.valloc("HS_init")
            i1 = self.valloc("HB_init")
            k.op("dve", lambda e: e.tensor_scalar(out=self.VEC[:, i0], in0=msl(1), scalar1=1.0, scalar2=None, op0=ALU.add),
                 reads=["M"], writes=["VEC"])
            k.op("dve", lambda e: e.tensor_copy(out=self.VEC[:, i1], in_=msl(0)), reads=["M"], writes=["VEC"])
        else:
            hs_hb(l - 1, 2, 0)
        for j in range(3):
            if part == 0 and j > 0:
                break
            fac = (0.5 if j != 1 else 1.0) / ALPHA
            ig = self.valloc(("GP", l, j))
            k.op("dve", lambda e: e.tensor_scalar(out=self.VEC[:, ig], in0=msl(3 * j + 2), scalar1=fac, scalar2=None, op0=ALU.mult),
                 reads=["M"], writes=["VEC"])
        if part == 0:
            return
        hs_hb(l, 0, 1)
        hs_hb(l, 1, 2)

    def ffn(self, l, f, j, tiles):
        k = self.k
        PS = self.PS
        G0 = 5
        self._ffn_it = 0

        def phase_a(jj, slot, wv, ti, t0, tn):
            it = self._ffn_it
            self._ffn_it += 1
            gb = it % 2
            ub = 2 + it % 2
            sgb = it % 2

            def mm(e):
                for kc in range(8):
                    e.matmul(PS[gb][:, :tn], lhsT=wv[:, kc, 0:128], rhs=self.H[:, kc, t0:t0 + tn], start=(kc == 0), stop=(kc == 7))
                for kc in range(8):
                    ins = e.matmul(PS[ub][:, :tn], lhsT=wv[:, kc, 128:256], rhs=self.H[:, kc, t0:t0 + tn], start=(kc == 0), stop=(kc == 7))
                return ins
            k.op("pe", mm, reads=[("W", slot), ("H", ti)], writes=[("PS", gb), ("PS", ub)])
            sg = self.SCR[:, sgb * 512:sgb * 512 + tn]
            k.op("act", lambda e: e.activation(out=sg, in_=PS[gb][:, :tn], func=AF.Silu), writes=[("PS", gb), ("SCR", sgb)])
            k.op("dve", lambda e: e.tensor_tensor(out=self.Aview(jj)[:, t0:t0 + tn], in0=sg, in1=PS[ub][:, :tn], op=ALU.mult),
                 reads=[("SCR", sgb)], writes=[("PS", ub), ("A", jj, ti)])

        def load1(jg):
            slot, w = self.wload(self.d["ffw1"][l, f, jg], 8 * 256)
            return slot, w.rearrange("p (kc n) -> p kc n", kc=8)

        for hf in range(2):
            jj0 = 0
            if hf == 0:
                grp = [load1(jj) for jj in range(G0)]
                for (ti, t0, tn, s) in tiles:
                    for jj in range(G0):
                        phase_a(jj, grp[jj][0], grp[jj][1], ti, t0, tn)
                for _ in range(G0):
                    self.bg_step()
                jj0 = G0
            for jj in range(jj0, NJH):
                slot, wv = load1(hf * NJH + jj)
                for (ti, t0, tn, s) in tiles:
                    phase_a(jj, slot, wv, ti, t0, tn)
                self.bg_step()
            for dc in range(NCH):
                slot, w = self.wload(self.d["ffw2"][l, f, hf, dc], NJH * 128)
                wv = w.rearrange("p (jj n) -> p jj n", jj=NJH)
                for (ti, t0, tn, s) in tiles:
                    yb = 4 + self._ffn_it % 2
                    self._ffn_it += 1

                    def mm2(e):
                        for jj in range(NJH):
                            ins = e.matmul(PS[yb][:, :tn], lhsT=wv[:, jj, :], rhs=self.Aview(jj)[:, t0:t0 + tn],
                                           start=(jj == 0), stop=(jj == NJH - 1))
                        return ins
                    k.op("pe", mm2, reads=[("W", slot)] + [("A", jj, ti) for jj in range(NJH)], writes=[("PS", yb)])
                    self.x_update(l, j, dc, ti, t0, tn, s, yb)
                self.bg_step()

    def x_update(self, l, j, dc, ti, t0, tn, s, yb):
        gp = self.V(("GP", l, j), dc, s)
        self.k.op("dve", lambda e: e.scalar_tensor_tensor(
            out=self.X[:, dc, t0:t0 + tn], in0=self.PS[yb][:, :tn], scalar=gp, in1=self.X[:, dc, t0:t0 + tn],
            op0=ALU.mult, op1=ALU.add),
            reads=["VEC"], writes=[("PS", yb), ("X", dc, ti)])

    def post_norm(self, l, j, tiles, make_h=True):
        k = self.k
        PS = self.PS
        xb, xq, akeys = self.atail()
        n = len(tiles)

        def banks(i):
            return (6, 7) if i % 2 == 0 else (4, 5)

        def stA(i):
            (ti, t0, tn, s) = tiles[i]
            xkeys = [("X", c, ti) for c in range(NCH)]
            xs = self.X[:, :, t0:t0 + tn]
            bm, bq = banks(i)
            k.op("act", lambda e: e.activation(out=xb[:, :, :tn], in_=xs, func=AF.Copy), reads=xkeys, writes=akeys)
            k.op("act", lambda e: e.activation(out=xq[:, :, :tn], in_=xs, func=AF.Square), reads=xkeys, writes=akeys)

            def mm(e):
                for c in range(NCH):
                    e.matmul(PS[bm][:, :tn], lhsT=self.ONESD[:], rhs=xb[:, c, :tn], start=(c == 0), stop=(c == NCH - 1))
                for c in range(NCH):
                    ins = e.matmul(PS[bq][:, :tn], lhsT=self.ONESD[:], rhs=xq[:, c, :tn], start=(c == 0), stop=(c == NCH - 1))
                return ins
            k.op("pe", mm, reads=akeys + ["ONESD"], writes=[("PS", bm), ("PS", bq)])

        def stBC(i):
            (ti, t0, tn, s) = tiles[i]
            xkeys = [("X", c, ti) for c in range(NCH)]
            xs = self.X[:, :, t0:t0 + tn]
            bm, bq = banks(i)
            T1, RS, NM = self.T1[:, :tn], self.RS[:, :tn], self.NM[:, :tn]
            k.op("act", lambda e: e.activation(out=T1, in_=PS[bm][:, :tn], func=AF.Square), writes=[("PS", bm), "T1"])
            k.op("dve", lambda e: e.tensor_tensor(out=T1, in0=PS[bq][:, :tn], in1=T1, op=ALU.subtract), writes=[("PS", bq), "T1"])
            k.op("act", lambda e: e.activation(out=T1, in_=T1, func=AF.Ln, bias=self.EPS[:, 0:1], scale=1.0), reads=["EPS"], writes=["T1"])
            k.op("act", lambda e: e.activation(out=RS, in_=T1, func=AF.Exp, scale=-0.5), reads=["T1"], writes=["RS"])
            k.op("dve", lambda e: e.scalar_tensor_tensor(out=NM, in0=PS[bm][:, :tn], scalar=-1.0, in1=RS, op0=ALU.mult, op1=ALU.mult),
                 reads=["RS"], writes=[("PS", bm), "NM"])
            k.op("dve", lambda e: e.tensor_tensor(out=xs, in0=xs, in1=bcast_mid(RS, NCH), op=ALU.mult), reads=["RS"], writes=xkeys)
            k.op("dve", lambda e: e.tensor_tensor(out=xs, in0=xs, in1=bcast_mid(NM, NCH), op=ALU.add), reads=["NM"], writes=xkeys)

        def stC(i):
            (ti, t0, tn, s) = tiles[i]
            for c in range(NCH):
                xc = self.X[:, c, t0:t0 + tn]
                if make_h:
                    k.op("dve", lambda e: e.tensor_scalar(
                        out=self.H[:, c, t0:t0 + tn], in0=xc, scalar1=self.V(("HS", l, j), c, s),
                        scalar2=self.V(("HB", l, j), c, s), op0=ALU.mult, op1=ALU.add),
                        reads=[("X", c, ti), "VEC"], writes=[("H", ti)])
                k.op("act", lambda e: e.activation(out=xc, in_=xc, func=AF.Identity,
                                                   bias=self.LNB[:, l, j, c:c + 1], scale=self.LNG[:, l, j, c:c + 1]),
                     reads=["LNG", "LNB"], writes=[("X", c, ti)])

        stA(0)
        for i in range(n):
            stBC(i)
            if i + 1 < n:
                stA(i + 1)
            stC(i)


    def mixer(self, l, tiles):
        kind = l % 3
        if kind == 0:
            self.na_mixer(l, l // 3, tiles)
        elif kind == 1:
            self.lru_mixer(l, tiles)
        else:
            self.sc_mixer(l, tiles)

    def out_proj(self, l, wdram, src, src_keys, tiles):
        k = self.k
        PS = self.PS
        it = 0
        for dc in range(NCH):
            slot, w = self.wload(wdram[dc], 8 * 128)
            wv = w.rearrange("p (kc n) -> p kc n", kc=8)
            for (ti, t0, tn, s) in tiles:
                yb = 4 + it % 2
                it += 1

                def mm(e):
                    for kc in range(8):
                        ins = e.matmul(PS[yb][:, :tn], lhsT=wv[:, kc, :], rhs=src(kc)[:, t0:t0 + tn],
                                       start=(kc == 0), stop=(kc == 7))
                    return ins
                k.op("pe", mm, reads=[("W", slot)] + src_keys(ti), writes=[("PS", yb)])
                self.x_update(l, 1, dc, ti, t0, tn, s, yb)


    def fence(self, keys):
        self.k.op("dve", lambda e: e.memset(self.STAT[:, 31:32], 0.0), writes=list(keys) + ["FENCE"])

    def na_mixer(self, l, idx, tiles):
        k = self.k
        PS = self.PS
        PS2 = self.PS2
        d = self.d
        need_ctx = (len(tiles) == 5)
        scale = HD ** -0.5
        WRf = self.WR[:, 2:6, :].rearrange("p s n -> p (s n)")
        bufs = [
            dict(QT=self.Aview(8), KT=self.Aview(9), VT=self.Aview(10).rearrange("p (tk n) -> p tk n", tk=18),
                 kQ=[("A", 8, t) for t in range(5)], kK=[("A", 9, t) for t in range(5)], kV=[("A", 10, t) for t in range(5)]),
            dict(QT=WRf[:, 0:NT], KT=WRf[:, NT:2 * NT], VT=WRf[:, 2 * NT:3 * NT].rearrange("p (tk n) -> p tk n", tk=18),
                 kQ=[("W", 2), ("W", 3)], kK=[("W", 3), ("W", 4)], kV=[("W", 4), ("W", 5)]),
        ]
        SB16 = self.SCR[:].bitcast(BF16)
        L16 = self.LNT[:].bitcast(BF16)
        NPB = 4
        PB = [SB16[:, b * 896:(b + 1) * 896] for b in range(NPB)]
        EW = SB16[:, 3584:3584 + 960].rearrange("p (r c) -> p r c", r=15)
        PTS = [L16[:, 1920 + b * 896:1920 + (b + 1) * 896].rearrange("p (g q) -> p g q", g=7) for b in range(2)]
        EB = self.LNT[:, 0:960].rearrange("p (r c) -> p r c", r=15)
        wkeys = [("W", i) for i in range(RING_SLOTS)]
        guard = [("SCR", i) for i in range(5)] + ["T1", "T2", "RS", "NM"] + wkeys
        na_keys = [("NA", "PB", i) for i in range(NPB)] + [("NA", "PTS", 0), ("NA", "PTS", 1), ("NA", "EW")]
        self.fence(guard + na_keys)
        self.ring_limit = 2
        self.ring_next = 0
        for b in range(NPB):
            k.op("dve", lambda e: e.memset(PB[b][:, 0:64], 0.0), writes=[("NA", "PB", b)])
            k.op("dve", lambda e: e.memset(PB[b][:, 576:640], 0.0), writes=[("NA", "PB", b)])
        k.dma("sp", self.CMASK[:], d["cmask"], writes=["CMASK"])
        rp = d["rpbt"]
        groups = [("lat", r) for r in range(ROWS)] + ([("ctx", g) for g in range(4)] if need_ctx else [])
        NG = len(groups)
        gcount = 0
        PJ = 7

        def proj_jobs(pr):
            B = bufs[pr % 2]
            jobs = []
            st = {}

            def ld(which):
                def run():
                    slot, w = self.wload(d["naw1"][idx, pr, which], 1024)
                    st[which] = (slot, w.rearrange("p (kc n) -> p kc n", kc=8))
                return run
            for (which, dst, dkeys, tl) in ((0, B["QT"], B["kQ"], tiles), (1, B["KT"], B["kK"], TILES)):
                jobs.append(ld(which))
                for (ti, t0, tn, s_) in tl:
                    def run(which=which, dst=dst, dkeys=dkeys, ti=ti, t0=t0, tn=tn):
                        wslot, wmat = st[which]

                        def mm(e):
                            for kc in range(8):
                                ins = e.matmul(PS[PJ][:, :tn], lhsT=wmat[:, kc, :], rhs=self.H[:, kc, t0:t0 + tn], start=(kc == 0), stop=(kc == 7))
                            return ins
                        k.op("pe", mm, reads=[("W", wslot), ("H", ti)], writes=[("PS", PJ)])
                        if which == 0:
                            k.op("act", lambda e: e.activation(out=dst[:, t0:t0 + tn], in_=PS[PJ][:, :tn], func=AF.Copy, scale=scale),
                                 writes=[("PS", PJ)] + dkeys)
                        else:
                            k.op("dve", lambda e: e.tensor_copy(out=dst[:, t0:t0 + tn], in_=PS[PJ][:, :tn]), writes=[("PS", PJ)] + dkeys)
                    jobs.append(run)
            jobs.append(ld(2))
            for g in range(5):
                def runv(g=g):
                    sv, wv = st[2]
                    ntk = 4 if g < 4 else 2

                    def mmv(e):
                        for q in range(ntk):
                            tk = 4 * g + q
                            for kc in range(8):
                                ins = e.matmul(PS[PJ][:, q * 128:(q + 1) * 128], lhsT=self.H[:, kc, tk * 128:(tk + 1) * 128], rhs=wv[:, kc, :],
                                               start=(kc == 0), stop=(kc == 7))
                        return ins
                    k.op("pe", mmv, reads=[("W", sv), ("H", g)], writes=[("PS", PJ)])
                    k.op("act", lambda e: e.activation(out=B["VT"][:, 4 * g:4 * g + ntk, :],
                                                       in_=PS[PJ][:, 0:ntk * 128].rearrange("p (q n) -> p q n", q=ntk), func=AF.Copy),
                         writes=[("PS", PJ)] + B["kV"])
                jobs.append(runv)
            return jobs

        for jb in proj_jobs(0):
            jb()
        for pr in range(NCH):
            B = bufs[pr % 2]
            QT, KT, VT, kQ, kK, kV = B["QT"], B["KT"], B["VT"], B["kQ"], B["kK"], B["kV"]
            for par in range(2):
                h = 2 * pr + par
                k.dma("sp", self.LNT[par * 64:(par + 1) * 64, 0:960], rp[idx, h], writes=["T1", "T2"])
            k.op("act", lambda e: e.activation(out=EB, in_=EB, func=AF.Exp), reads=[], writes=["T1", "T2"])
            k.op("dve", lambda e: e.tensor_tensor(out=EW, in0=EB, in1=self.CMASK[:].unsqueeze(1).broadcast_to([128, 15, 64]), op=ALU.mult),
                 reads=["T1", "T2", "CMASK"], writes=[("NA", "EW")])
            if pr + 1 < NCH:
                self.bg.extend(proj_jobs(pr + 1))
            info = []
            for gi, (kind, r) in enumerate(groups):
                gg = gcount + gi
                lat = (kind == "lat")
                r0 = min(max(r - WIN_R // 2, 0), ROWS - WIN_R) if lat else 0
                qc0 = r * GRID_W if lat else TL + r * 64
                if lat:
                    if r0 % 2 == 0:
                        blocks = [(64 + 128 * i, r0 // 2 + i) for i in range(4)]
                    else:
                        blocks = [(128 * i, (r0 - 1) // 2 + i) for i in range(5)]
                else:
                    blocks = []
                blocks = blocks + [(640, 16), (768, 17)]
                info.append(dict(gi=gi, lat=lat, r=r, r0=r0, qc0=qc0, blocks=blocks, sb=gg % 2, pbi=gg % NPB, sti=gg % 4))
            gcount += NG

            def st1(g):
                S = PS2[g["sb"]]
                kS = [("PS", 2 * g["sb"]), ("PS", 2 * g["sb"] + 1)]
                st_ = self.STAT[:, g["sti"] * 8:g["sti"] * 8 + 8]
                qc0, r0 = g["qc0"], g["r0"]

                def mms(e):
                    for par in range(2):
                        ps_ = slice(par * 64, (par + 1) * 64)
                        if g["lat"]:
                            e.matmul(S[ps_, 0:512], lhsT=QT[ps_, qc0:qc0 + 64], rhs=KT[ps_, r0 * 64:r0 * 64 + 512], start=True, stop=True)
                        ins = e.matmul(S[ps_, 512:768], lhsT=QT[ps_, qc0:qc0 + 64], rhs=KT[ps_, TL:NT], start=True, stop=True)
                    return ins
                k.op("pe", mms, reads=kQ + kK, writes=kS)
                lo = 0 if g["lat"] else 512
                k.op("dve", lambda e: e.reduce_max(out=st_[:, 0:1], in_=S[:, lo:768], axis=mybir.AxisListType.X, negate=True),
                     writes=kS + [("ST", g["sti"])])

            def st2(g):
                S = PS2[g["sb"]]
                kS = [("PS", 2 * g["sb"]), ("PS", 2 * g["sb"] + 1)]
                st_ = self.STAT[:, g["sti"] * 8:g["sti"] * 8 + 8]
                pbuf = PB[g["pbi"]]
                kpb = ("NA", "PB", g["pbi"])
                if g["lat"]:
                    k.op("act", lambda e: e.activation(out=pbuf[:, 64:576], in_=S[:, 0:512], func=AF.Exp, bias=st_[:, 0:1], scale=1.0),
                         reads=[("ST", g["sti"])], writes=kS + [kpb])
                k.op("act", lambda e: e.activation(out=pbuf[:, 640:896], in_=S[:, 512:768], func=AF.Exp, bias=st_[:, 0:1], scale=1.0,
                                                   accum_out=st_[:, 1:2]),
                     reads=[], writes=kS + [kpb, ("ST", g["sti"])])

            def st3(g):
                st_ = self.STAT[:, g["sti"] * 8:g["sti"] * 8 + 8]
                kst = ("ST", g["sti"])
                pbuf = PB[g["pbi"]]
                kpb = ("NA", "PB", g["pbi"])
                if g["lat"]:
                    d0 = g["r0"] - g["r"] + 7
                    pw = pbuf[:, 64:576].rearrange("p (r c) -> p r c", r=8)
                    k.op("dve", lambda e: e.scalar_tensor_tensor(out=pw, in0=pw, scalar=1.0, in1=EW[:, d0:d0 + 8, :], op0=ALU.mult, op1=ALU.mult,
                                                                 accum_out=st_[:, 2:3]),
                         reads=[("NA", "EW")], writes=[kpb, kst])
                    k.op("dve", lambda e: e.tensor_tensor(out=st_[:, 3:4], in0=st_[:, 1:2], in1=st_[:, 2:3], op=ALU.add), writes=[kst])
                    k.op("dve", lambda e: e.reciprocal(out=st_[:, 4:5], in_=st_[:, 3:4]), writes=[kst])
                else:
                    k.op("dve", lambda e: e.reciprocal(out=st_[:, 4:5], in_=st_[:, 1:2]), writes=[kst])

            def st3b(g):
                st_ = self.STAT[:, g["sti"] * 8:g["sti"] * 8 + 8]
                kst = ("ST", g["sti"])
                pbuf = PB[g["pbi"]]
                kpb = ("NA", "PB", g["pbi"])
                lo2 = 64 if g["lat"] else 640
                k.op(self.na_norm_eng, lambda e: e.tensor_scalar(out=pbuf[:, lo2:896], in0=pbuf[:, lo2:896], scalar1=st_[:, 4:5], scalar2=None,
                                                                 op0=ALU.mult), reads=[kst], writes=[kpb])

            def st4(g):
                b = g["sb"]
                PTv = PS[4 + b].bitcast(BF16)
                pbuf = PB[g["pbi"]]

                def mmt(e):
                    for bi, (col, tk) in enumerate(g["blocks"]):
                        ins = e.transpose(out=PTv[:, bi * 128:(bi + 1) * 128], in_=pbuf[:, col:col + 128], identity=self.IDB[:])
                    return ins
                k.op("pe", mmt, reads=[("NA", "PB", g["pbi"]), "IDB"], writes=[("PS", 4 + b)])

            def st5(g):
                b = g["sb"]
                PTv = PS[4 + b].bitcast(BF16)
                nblk = len(g["blocks"])
                k.op("act", lambda e: e.activation(out=PTS[b][:, 0:nblk, :], in_=PTv[:, 0:nblk * 128].rearrange("p (g q) -> p g q", g=nblk),
                                                   func=AF.Copy),
                     writes=[("PS", 4 + b), ("NA", "PTS", b)])

            def st6(g):
                b = g["sb"]
                gi = g["gi"]
                blk8 = gi // 8
                ob = 6
                oc = (gi % 8) * 64
                nblk = len(g["blocks"])

                def mmo(e):
                    for par in range(2):
                        ps_ = slice(par * 64, (par + 1) * 64)
                        for bi, (col, tk) in enumerate(g["blocks"]):
                            ins = e.matmul(PS[ob][ps_, oc:oc + 64], lhsT=VT[:, tk, ps_], rhs=PTS[b][:, bi, ps_],
                                           start=(bi == 0), stop=(bi == nblk - 1))
                    return ins
                k.op("pe", mmo, reads=kV + [("NA", "PTS", b)], writes=[("PS", ob)])
                if (gi % 8 == 7) or (gi == NG - 1):
                    ncols = oc + 64
                    tok0 = blk8 * 512
                    k.op("act", lambda e: e.activation(out=self.Aview(pr)[:, tok0:tok0 + ncols], in_=PS[ob][:, 0:ncols], func=AF.Copy),
                         writes=[("PS", ob), ("A", pr, blk8)])

            stages = [st1, st2, st3, st3b, st4, st5, st6]
            for t in range(NG + len(stages) - 1):
                for si, fn in enumerate(stages):
                    gi = t - si
                    if 0 <= gi < NG:
                        fn(info[gi])
                self.bg_step()
            self.bg_flush()
        self.fence(guard + na_keys)
        self.ring_limit = RING_SLOTS
        self.ring_next = 0
        self.out_proj(l, d["naw2"][idx], lambda kc: self.Aview(kc), lambda ti: [("A", kc, ti) for kc in range(8)], tiles)

    def lru_prep(self):
        k = self.k
        LV = self.LV
        k.dma("sp", LV[:, :, 0:11], self.d["lruvec"], writes=["LV"])
        lam = LV[:, :, 9:11]
        e = LV[:, :, 19:21]
        p = LV[:, :, 21:23]
        sp = LV[:, :, 15:17]
        rw = dict(reads=["LV", "CONST"], writes=["LV"])
        k.op("act", lambda a: a.activation(out=e, in_=lam, func=AF.Exp, scale=-1.0), **rw)
        k.op("dve", lambda v: v.tensor_scalar(out=p, in0=e, scalar1=-1.0 / 6.0, scalar2=None, op0=ALU.mult), **rw)
        for cst in (1.0 / 5.0, -1.0 / 4.0, 1.0 / 3.0, -1.0 / 2.0, 1.0):
            k.op("dve", lambda v, cst=cst: v.scalar_tensor_tensor(out=p, in0=p, scalar=cst, in1=e, op0=ALU.add, op1=ALU.mult), **rw)
        k.op("act", lambda a: a.activation(out=sp, in_=e, func=AF.Ln, bias=self.CONST[0:BW, 0:1], scale=1.0), **rw)
        k.op("dve", lambda v: v.tensor_tensor(out=p, in0=p, in1=sp, op=ALU.subtract), **rw)
        k.op("dve", lambda v: v.tensor_single_scalar(out=e, in_=e, scalar=0.1, op=ALU.is_lt), **rw)
        k.op("dve", lambda v: v.tensor_tensor(out=p, in0=p, in1=e, op=ALU.mult), **rw)
        k.op("dve", lambda v: v.tensor_tensor(out=sp, in0=sp, in1=p, op=ALU.add), **rw)
        k.op("dve", lambda v: v.tensor_scalar(out=LV[:, :, 17:19], in0=sp, scalar1=-8.0, scalar2=None, op0=ALU.mult), **rw)
        k.op("dve", lambda v: v.tensor_scalar(out=LV[:, :, 15:17], in0=sp, scalar1=-4.0, scalar2=None, op0=ALU.mult), **rw)
        k.op("dve", lambda v: v.tensor_scalar(out=LV[:, :, 11:15], in0=LV[:, :, 5:9], scalar1=0.5, scalar2=None, op0=ALU.mult), **rw)

    def lru_mixer(self, l, tiles):
        k = self.k
        PS = self.PS
        d = self.d
        LV = self.LV
        P = BW
        self.lru_prep()
        ntok = NT if len(tiles) == 5 else TL
        has_ctx = (len(tiles) == 5)
        seqs = [(0, TL)] + ([(TL, NT)] if has_ctx else [])
        NTI = len(tiles)

        def arr(i):
            return self.A[0:P, i * 2 * NT:(i + 1) * 2 * NT].bitcast(F32)

        def kt(i, ti):
            return ("LA", i, ti)

        def kall(i):
            return [("LA", i, t) for t in range(5)]
        U32, AA, E1, E2, E3 = [arr(i) for i in range(5)]
        iU32, iAA, iE1, iE2, iE3, iU16 = 0, 1, 2, 3, 4, 5
        U16 = self.A[0:P, 10 * NT:11 * NT]
        Y16 = self.SCR[:].bitcast(BF16)
        T1 = self.LNT[0:P, 0:512]
        T2 = self.LNT[0:P, 512:1024]
        T3 = self.LNT[0:P, 1024:1536]
        T4 = self.LNT[0:P, 1536:2048]
        allA = [("A", jj, t) for jj in range(NJH) for t in range(5)]
        allL = [("LA", i, t) for i in range(6) for t in range(5)] + [("LY", h_, t) for h_ in range(2) for t in range(5)]
        guard = allA + [("SCR", i) for i in range(5)] + ["T1", "T2", "RS", "NM"]
        self.fence(guard + allL)
        fwd_order = ([tiles[4]] if has_ctx else []) + list(tiles[:4])
        bwd_order = ([tiles[4]] if has_ctx else []) + list(reversed(tiles[:4]))

        def rev(ap2d, a, b):
            sl_ = ap2d[:, a:b]
            pa = sl_.ap
            return bass.AP(sl_.tensor, sl_.offset + (b - a - 1) * pa[-1][0], [list(pa[0]), [-pa[-1][0], b - a]])
        it = 0
        self._lru_it = 0
        w2slabs = {}
        blk = {}

        def lvn(n, i):
            return LV[:, n, i:i + 1]

        def steps12(n):
            nonlocal it
            def lv(i):
                return lvn(n, i)
            s1, w1 = self.wload(d["lruw1"][n], 8 * 2 * P)
            w1v = w1.rearrange("p (kc g n) -> p kc g n", kc=8, g=2)
            sg_, wg = self.wload(d["lrug"][n], 4 * P, parts=P)
            wgv = wg.rearrange("p (q n) -> p q n", q=4)

            for (ti, t0, tn, s) in tiles:
                pb = it % 2
                it += 1

                def mm(e):
                    for kc in range(8):
                        ins = e.matmul(PS[pb][0:P, :tn], lhsT=w1v[:, kc, 1, :], rhs=self.H[:, kc, t0:t0 + tn], start=(kc == 0), stop=(kc == 7))
                    return ins
                k.op("pe", mm, reads=[("W", s1), ("H", ti)], writes=[("PS", pb)])
                k.op("act", lambda e: e.activation(out=AA[:, t0:t0 + tn], in_=PS[pb][0:P, :tn], func=AF.Copy), writes=[("PS", pb), kt(iAA, ti)])
            k.op("act", lambda e: e.activation(out=U32[:, 0:ntok], in_=AA[:, 0:ntok], func=AF.Identity, scale=lv(2), bias=lv(4)),
                 reads=kall(iAA) + ["LV"], writes=kall(iU32))
            for (a, b) in seqs:
                for (wi, sh) in ((0, 2), (1, 1)):
                    k.op("dve", lambda e: e.scalar_tensor_tensor(out=U32[:, a + sh:b], in0=AA[:, a:b - sh], scalar=lv(wi), in1=U32[:, a + sh:b],
                                                                 op0=ALU.mult, op1=ALU.add), reads=kall(iAA) + ["LV"], writes=kall(iU32))
                k.op("dve", lambda e: e.scalar_tensor_tensor(out=U32[:, a:b - 1], in0=AA[:, a + 1:b], scalar=lv(3), in1=U32[:, a:b - 1],
                                                             op0=ALU.mult, op1=ALU.add), reads=kall(iAA) + ["LV"], writes=kall(iU32))
            k.op("act", lambda e: e.activation(out=U16[:, 0:ntok], in_=U32[:, 0:ntok], func=AF.Copy), reads=kall(iU32), writes=kall(iU16))
            blk[n] = (s1, w1v, sg_, wgv)

        def steps3(n):
            nonlocal it
            (s1, w1v, sg_, wgv) = blk[n]

            def lv(i):
                return lvn(n, i)
            for dr in range(2):
                EA, iEA = (E1, iE1) if dr == 0 else (E3, iE3)
                pa_order = bwd_order if dr == 0 else fwd_order
                gsets = []
                for (ti, t0, tn, s) in pa_order:
                    gsets.append((it % 2, 2 + it % 2))
                    it += 1

                def stG(i):
                    (ti, t0, tn, s) = pa_order[i]
                    rb, ib = gsets[i]

                    def mmg(e):
                        e.matmul(PS[rb][0:P, :tn], lhsT=wgv[:, 2 * dr, :], rhs=U16[:, t0:t0 + tn], start=True, stop=True)
                        return e.matmul(PS[ib][0:P, :tn], lhsT=wgv[:, 2 * dr + 1, :], rhs=U16[:, t0:t0 + tn], start=True, stop=True)
                    k.op("pe", mmg, reads=[("W", sg_), kt(iU16, ti)], writes=[("PS", rb), ("PS", ib)])

                def stE(i):
                    (ti, t0, tn, s) = pa_order[i]
                    rb, ib = gsets[i]
                    ta, tak = (T1, "T1") if rb == 0 else (T3, "RS")
                    tb, tbk = (T2, "T2") if rb == 0 else (T4, "NM")
                    k.op("act", lambda e: e.activation(out=ta[:, :tn], in_=PS[rb][0:P, :tn], func=AF.Tanh, scale=0.5, bias=lv(11 + 2 * dr)),
                         reads=["LV"], writes=[("PS", rb), tak])
                    k.op("act", lambda e: e.activation(out=tb[:, :tn], in_=PS[ib][0:P, :tn], func=AF.Tanh, scale=0.5, bias=lv(12 + 2 * dr)),
                         reads=["LV"], writes=[("PS", ib), tbk])
                    k.op("act", lambda e: e.activation(out=AA[:, t0:t0 + tn], in_=ta[:, :tn], func=AF.Exp, scale=lv(15 + dr), bias=lv(15 + dr)),
                         reads=[tak, "LV"], writes=[kt(iAA, ti)])
                    k.op("act", lambda e: e.activation(out=EA[:, t0:t0 + tn], in_=ta[:, :tn], func=AF.Exp, scale=lv(17 + dr), bias=lv(17 + dr)),
                         reads=[tak, "LV"], writes=[kt(iEA, ti)])
                    k.op("dve", lambda e: e.scalar_tensor_tensor(out=E2[:, t0:t0 + tn], in0=tb[:, :tn], scalar=1.0, in1=U32[:, t0:t0 + tn],
                                                                 op0=ALU.add, op1=ALU.mult), reads=[tbk, kt(iU32, ti)], writes=[kt(iE2, ti)])
                stG(0)
                for i in range(len(pa_order)):
                    if i + 1 < len(pa_order):
                        stG(i + 1)
                    stE(i)
                    self.bg_step(4)
                k.op("act", lambda e: e.activation(out=EA[:, 0:ntok], in_=EA[:, 0:ntok], func=AF.Sqrt, scale=-0.25, bias=self.CONST[0:P, 1:2]),
                     reads=["CONST"], writes=kall(iEA)[:NTI])
                sc_order = fwd_order if dr == 0 else bwd_order
                prev = None
                for (ti, t0, tn, s) in sc_order:
                    k.op("dve", lambda e: e.tensor_tensor(out=E2[:, t0:t0 + tn], in0=E2[:, t0:t0 + tn], in1=EA[:, t0:t0 + tn], op=ALU.mult),
                         reads=[kt(iEA, ti)], writes=[kt(iE2, ti)])
                    if prev is None:
                        init = 0.0
                        rk = []
                    else:
                        (pti, pt0, ptn, ps_) = prev
                        init = EA[:, pt0 + ptn - 1:pt0 + ptn] if dr == 0 else EA[:, pt0:pt0 + 1]
                        rk = [kt(iEA, pti)]
                    if dr == 0:
                        k.op("dve", lambda e: e.tensor_tensor_scan(out=EA[:, t0:t0 + tn], data0=AA[:, t0:t0 + tn], data1=E2[:, t0:t0 + tn],
                                                                   initial=init, op0=ALU.mult, op1=ALU.add),
                             reads=[kt(iAA, ti), kt(iE2, ti)] + rk, writes=[kt(iEA, ti)])
                    else:
                        k.op("dve", lambda e: e.tensor_tensor_scan(out=rev(EA, t0, t0 + tn), data0=rev(AA, t0, t0 + tn),
                                                                   data1=rev(E2, t0, t0 + tn), initial=init, op0=ALU.mult, op1=ALU.add),
                             reads=[kt(iAA, ti), kt(iE2, ti)] + rk, writes=[kt(iEA, ti)])
                    prev = (ti, t0, tn, s)

        def step4(n):
            nonlocal it
            (s1, w1v, sg_, wgv) = blk[n]
            self.bg_flush()
            yh = n % 2
            yo = yh * NT
            for (ti, t0, tn, s) in bwd_order:
                pb = it % 2
                it += 1
                k.op("dve", lambda e: e.tensor_tensor(out=E1[:, t0:t0 + tn], in0=E1[:, t0:t0 + tn], in1=E3[:, t0:t0 + tn], op=ALU.add),
                     reads=[kt(iE3, ti)], writes=[kt(iE1, ti)])

                def mm4(e):
                    for kc in range(8):
                        ins = e.matmul(PS[pb][0:P, :tn], lhsT=w1v[:, kc, 0, :], rhs=self.H[:, kc, t0:t0 + tn], start=(kc == 0), stop=(kc == 7))
                    return ins
                k.op("pe", mm4, reads=[("W", s1), ("H", ti)], writes=[("PS", pb)])
                tg, tgk = (T1, "T1") if pb == 0 else (T3, "RS")
                k.op("act", lambda e: e.activation(out=tg[:, :tn], in_=PS[pb][0:P, :tn], func=AF.Gelu_apprx_tanh), writes=[("PS", pb), tgk])
                k.op("dve", lambda e: e.tensor_tensor(out=Y16[0:P, yo + t0:yo + t0 + tn], in0=tg[:, :tn], in1=E1[:, t0:t0 + tn], op=ALU.mult),
                     reads=[tgk, kt(iE1, ti)], writes=[("LY", yh, ti)])
            if n % 2 == 1:
                sa, wa = self.wload(d["lruw2"][n - 1], D, parts=P)
                sb_, wb = self.wload(d["lruw2"][n], D, parts=P)

                def mkjob(dc, ti, t0, tn, s, sa=sa, wa=wa, sb_=sb_, wb=wb):
                    def run():
                        yb = 4 + self._lru_it % 2
                        self._lru_it += 1

                        def mm5(e):
                            e.matmul(PS[yb][:, :tn], lhsT=wa[:, dc * 128:(dc + 1) * 128], rhs=Y16[0:P, t0:t0 + tn], start=True, stop=False)
                            return e.matmul(PS[yb][:, :tn], lhsT=wb[:, dc * 128:(dc + 1) * 128], rhs=Y16[0:P, NT + t0:NT + t0 + tn],
                                            start=False, stop=True)
                        k.op("pe", mm5, reads=[("W", sa), ("W", sb_), ("LY", 0, ti), ("LY", 1, ti)], writes=[("PS", yb)])
                        self.x_update(l, 1, dc, ti, t0, tn, s, yb)
                    return run
                for dc in range(NCH):
                    for (ti, t0, tn, s) in tiles:
                        self.bg.append(mkjob(dc, ti, t0, tn, s))

        steps12(0)
        for n in range(NBLK):
            steps3(n)
            if n + 1 < NBLK:
                steps12(n + 1)
            step4(n)
        self.bg_flush()
        self.fence(guard + allL)

    def sc_mixer(self, l, tiles):
        k = self.k
        PS = self.PS
        d = self.d
        if not hasattr(self, "SCCW"):
            self.SCCW = self.st.enter_context(self.nc.sbuf_tensor("SCCW", [128, 3, NCH], F32))
        k.dma("sp", self.SCCW[:], d["sccw"], writes=["SCCW"])
        V = self.A[:, 8 * NT:10 * NT].bitcast(F32)
        C = self.SCR[:, 0:NT]
        guard = [("A", jj, ti) for jj in (8, 9, 10) for ti in range(5)]
        vk = [("SCV", ti) for ti in range(5)]
        self.fence(guard + vk)
        n = len(tiles)
        seq_of = lambda ti: (0, TL) if ti < 4 else (TL, NT)
        it = 0
        for c in range(NCH):
            sb_, wb = self.wload(d["scw1"][c, 0], 1024)
            sc_, wc = self.wload(d["scw1"][c, 1], 1024)
            su_, wu = self.wload(d["scw1"][c, 2], 1024)
            wb = wb.rearrange("p (kc n) -> p kc n", kc=8)
            wc = wc.rearrange("p (kc n) -> p kc n", kc=8)
            wu = wu.rearrange("p (kc n) -> p kc n", kc=8)
            w0 = self.SCCW[:, 0, c:c + 1]
            w1 = self.SCCW[:, 1, c:c + 1]
            w2 = self.SCCW[:, 2, c:c + 1]

            def stage1(i):
                nonlocal it
                (ti, t0, tn, s) = tiles[i]
                cb = it % 2
                ub = 2 + it % 2
                it += 1

                def mm(e):
                    for kc in range(8):
                        e.matmul(PS[cb][:, :tn], lhsT=wc[:, kc, :], rhs=self.H[:, kc, t0:t0 + tn], start=(kc == 0), stop=(kc == 7))
                    for kc in range(8):
                        ins = e.matmul(PS[ub][:, :tn], lhsT=wu[:, kc, :], rhs=self.H[:, kc, t0:t0 + tn], start=(kc == 0), stop=(kc == 7))
                    return ins
                k.op("pe", mm, reads=[("W", sc_), ("W", su_), ("H", ti)], writes=[("PS", cb), ("PS", ub)])
                cg = self.LNT[:, cb * 512:cb * 512 + tn]
                cgk = ["T1", "T2"][cb]
                k.op("act", lambda e: e.activation(out=cg, in_=PS[cb][:, :tn], func=AF.Copy), writes=[("PS", cb), cgk])
                k.op("dve", lambda e: e.tensor_tensor(out=V[:, t0:t0 + tn], in0=cg, in1=PS[ub][:, :tn], op=ALU.mult),
                     reads=[cgk], writes=[("PS", ub), ("SCV", ti)])

            def stage2(i):
                nonlocal it
                (ti, t0, tn, s) = tiles[i]
                (sa, se) = seq_of(ti)
                ck = ("SCR", ti)
                k.op("act", lambda e: e.activation(out=C[:, t0:t0 + tn], in_=V[:, t0:t0 + tn], func=AF.Copy, scale=w1),
                     reads=[("SCV", ti), "SCCW"], writes=[ck])
                a1 = max(t0, sa + 1)
                rk = [("SCV", ti)] + ([("SCV", ti - 1)] if t0 > sa else [])
                k.op("dve", lambda e: e.scalar_tensor_tensor(out=C[:, a1:t0 + tn], in0=V[:, a1 - 1:t0 + tn - 1], scalar=w0, in1=C[:, a1:t0 + tn],
                                                             op0=ALU.mult, op1=ALU.add), reads=rk + ["SCCW"], writes=[ck])
                b2 = min(t0 + tn, se - 1)
                rk = [("SCV", ti)] + ([("SCV", ti + 1)] if t0 + tn < se else [])
                k.op("dve", lambda e: e.scalar_tensor_tensor(out=C[:, t0:b2], in0=V[:, t0 + 1:b2 + 1], scalar=w2, in1=C[:, t0:b2],
                                                             op0=ALU.mult, op1=ALU.add), reads=rk + ["SCCW"], writes=[ck])
                bb = 4 + it % 2
                it += 1

                def mm2(e):
                    for kc in range(8):
                        ins = e.matmul(PS[bb][:, :tn], lhsT=wb[:, kc, :], rhs=self.H[:, kc, t0:t0 + tn], start=(kc == 0), stop=(kc == 7))
                    return ins
                k.op("pe", mm2, reads=[("W", sb_), ("H", ti)], writes=[("PS", bb)])
                k.op("dve", lambda e: e.tensor_tensor(out=self.Aview(c)[:, t0:t0 + tn], in0=PS[bb][:, :tn], in1=C[:, t0:t0 + tn], op=ALU.mult),
                     reads=[ck], writes=[("PS", bb), ("A", c, ti)])

            for i in range(n):
                stage1(i)
                if i >= 1:
                    stage2(i - 1)
            stage2(n - 1)
        self.fence(guard + vk)
        self.out_proj(l, d["scw2"], lambda kc: self.Aview(kc), lambda ti: [("A", kc, ti) for kc in range(8)], tiles)


def _prep_shared(inp):
    f = np.float32
    sh = {}
    mw = np.asarray(inp["mod_w"], f)
    sh["modw"] = np.ascontiguousarray(mw.reshape(DEPTH, 8, 128, 36, 256).transpose(0, 3, 2, 1, 4)).reshape(DEPTH, 36, 128, 2048)
    sh["modb"] = np.ascontiguousarray(np.asarray(inp["mod_b"], f).reshape(DEPTH, 72, 128).transpose(2, 0, 1))
    sh["lng"] = np.ascontiguousarray(np.asarray(inp["ln_g"], f).reshape(DEPTH, 3, NCH, 128).transpose(3, 0, 1, 2))
    sh["lnb"] = np.ascontiguousarray(np.asarray(inp["ln_b"], f).reshape(DEPTH, 3, NCH, 128).transpose(3, 0, 1, 2))
    w1 = np.asarray(inp["ffn_w_in"], f)
    g = w1[..., :DFF].reshape(DEPTH, 2, 8, 128, NJ, 128)
    u = w1[..., DFF:].reshape(DEPTH, 2, 8, 128, NJ, 128)
    gu = np.stack([g, u], axis=5)
    sh["ffw1"] = np.ascontiguousarray(gu.transpose(0, 1, 4, 3, 2, 5, 6)).reshape(DEPTH, 2, NJ, 128, 2048)
    w2 = np.asarray(inp["ffn_w_out"], f)
    w2 = w2.reshape(DEPTH, 2, 2, NJH, 128, NCH, 128)
    sh["ffw2"] = np.ascontiguousarray(w2.transpose(0, 1, 2, 5, 4, 3, 6)).reshape(DEPTH, 2, 2, NCH, 128, NJH * 128)
    sh["ident"] = np.eye(128, dtype=f)
    w = np.asarray(inp["sc_w_in"], f)[0].reshape(8, 128, 3, NCH, 128)
    sh["scw1"] = np.ascontiguousarray(w.transpose(3, 2, 1, 0, 4)).reshape(NCH, 3, 128, 1024)
    w = np.asarray(inp["sc_w_out"], f)[0].reshape(8, 128, NCH, 128)
    sh["scw2"] = np.ascontiguousarray(w.transpose(2, 1, 0, 3)).reshape(NCH, 128, 1024)
    sh["sccw"] = np.ascontiguousarray(np.asarray(inp["sc_conv_w"], f)[0].reshape(3, NCH, 128).transpose(2, 0, 1))
    w = np.asarray(inp["lru_w_in"], f)[0]
    wg = w[:, :DRNN].reshape(8, 128, NBLK, BW)
    wu = w[:, DRNN:].reshape(8, 128, NBLK, BW)
    sh["lruw1"] = np.ascontiguousarray(np.stack([wg, wu], axis=3).transpose(2, 1, 0, 3, 4)).reshape(NBLK, 128, 8 * 2 * BW)
    g = np.asarray(inp["lru_w_gates"], f)[0]
    sh["lrug"] = np.ascontiguousarray(g.transpose(2, 3, 0, 1, 4)).reshape(NBLK, BW, 4 * BW)
    sh["lruw2"] = np.ascontiguousarray(np.asarray(inp["lru_w_out"], f)[0].reshape(NBLK, BW, D))
    lv = np.concatenate([np.asarray(inp["lru_conv_w"], f)[0],
                         np.asarray(inp["lru_conv_b"], f)[0][None],
                         np.asarray(inp["lru_b_gates"], f)[0].reshape(4, DRNN),
                         np.asarray(inp["lru_lambda"], f)[0]], axis=0)
    sh["lruvec"] = np.ascontiguousarray(lv.reshape(11, NBLK, BW).transpose(2, 1, 0))
    w = np.asarray(inp["na_w_qkv"], f).reshape(2, 8, 128, 3, NCH, 128)
    sh["naw1"] = np.ascontiguousarray(w.transpose(0, 4, 3, 2, 1, 5)).reshape(2, NCH, 3, 128, 1024)
    w = np.asarray(inp["na_w_o"], f).reshape(2, 8, 128, NCH, 128)
    sh["naw2"] = np.ascontiguousarray(w.transpose(0, 3, 2, 1, 4)).reshape(2, NCH, 128, 1024)
    qc = np.arange(64)
    dcol = np.clip(qc[None, :] - qc[:, None], 1 - WIN_C, WIN_C - 1) + WIN_C - 1
    rpb = np.asarray(inp["na_rpb"], f)
    sh["rpbt"] = np.ascontiguousarray(rpb[:, :, :, dcol].transpose(0, 1, 3, 2, 4)).reshape(2, HEADS, 64, 15 * 64)
    cs = np.clip(qc - WIN_C // 2, 0, GRID_W - WIN_C)
    cm = ((qc[None, :] >= cs[:, None]) & (qc[None, :] < cs[:, None] + WIN_C)).astype(f)
    sh["cmask"] = np.ascontiguousarray(np.concatenate([cm, cm], axis=0))
    return sh


def _prep_core(inp, b):
    f = np.float32
    x = np.asarray(inp["x"], f)[b]
    ctx = np.asarray(inp["ctx"], f)[b]
    xin = np.ascontiguousarray(np.concatenate([x.T, ctx.T], axis=1))
    cc = np.stack([np.asarray(inp["c"], f)[b], np.asarray(inp["c_ctx"], f)], axis=1)
    cc = np.ascontiguousarray(cc.reshape(NCH, 128, 2).transpose(1, 0, 2))
    return {"xin": xin, "cc": cc}


_CACHE = {}


def kernel(**inputs):
    n = 8
    prog = _CACHE.get("prog")
    if prog is None:
        prog = Prog()
        prog.build()
        _CACHE["prog"] = prog
    sh = _prep_shared(inputs)
    in_maps = []
    for b in range(n):
        m = dict(sh)
        m.update(_prep_core(inputs, b))
        in_maps.append(m)
    res = run_bass_kernel_spmd(prog.nc, in_maps, core_ids=list(range(n)))
    out = np.empty((n, TL, D), dtype=np.float32)
    for b in range(n):
        out[b] = res.results[b]["out"][:, :TL].T
    return out
```

```python
import numpy as np
from contextlib import ExitStack
import concourse.bass as bass
import concourse.mybir as mybir
from concourse.bass_utils import run_bass_kernel_spmd

F32 = mybir.dt.float32
BF16 = mybir.dt.bfloat16
AF = mybir.ActivationFunctionType
ALU = mybir.AluOpType

D = 1024
NCH = 8
TL = 2048
TC = 256
NT = TL + TC
DEPTH = 4
DFF = 2816
NJ = 22
NJH = 11
DRNN = 1408
NBLK = 16
BW = 88
ALPHA = (2 * DEPTH) ** 0.25
LN_EPS = 1e-5
EPS_P = LN_EPS / (ALPHA * ALPHA)
HEADS = 16
HD = 64
GRID_W = 64
ROWS = 32
WIN_R = 8
WIN_C = 16

ENGS = ("pe", "act", "dve", "pool", "sp")
N_DMA_SEMS = 24
N_SW_SEMS = 16
RING_SLOTS = 6
SLOT_ELEMS = 2048

TILES = [(0, 0, 512, 0), (1, 512, 512, 0), (2, 1024, 512, 0), (3, 1536, 512, 0), (4, 2048, 256, 1)]


class K:
    def __init__(self, nc, stack):
        self.nc = nc
        self.eng = {"pe": nc.tensor, "act": nc.scalar, "dve": nc.vector, "pool": nc.gpsimd, "sp": nc.sync}
        self.sem = {e: stack.enter_context(nc.semaphore("prog_" + e)) for e in ENGS}
        self.tick = {e: 0 for e in ENGS}
        self.dsem = [stack.enter_context(nc.semaphore("dma_%d" % i)) for i in range(N_DMA_SEMS)]
        self.dtick = [0] * N_DMA_SEMS
        self.dnext_sw = 0
        self.dnext_hw = 0
        self.seen = {e: {} for e in ENGS}
        self.clocks = {}
        self.last_w = {}
        self.readers = {}
        self.n_wait = 0
        self.n_ops = 0

    def _semof(self, src):
        return self.sem[src] if isinstance(src, str) else self.dsem[src]

    def _need(self, reads, writes):
        need = {}

        def add(src, t):
            if t > need.get(src, 0):
                need[src] = t

        for k in reads:
            lw = self.last_w.get(k)
            if lw:
                add(*lw)
        for k in writes:
            lw = self.last_w.get(k)
            if lw:
                add(*lw)
            rd = self.readers.get(k)
            if rd:
                for src, t in rd.items():
                    add(src, t)
        return need

    def _wait(self, e, need):
        seen = self.seen[e]
        for src, t in sorted(need.items(), key=lambda kv: str(kv[0])):
            if seen.get(src, 0) >= t:
                continue
            self.eng[e].wait_ge(self._semof(src), t)
            self.n_wait += 1
            seen[src] = t
            snap = self.clocks.get((src, t))
            if snap:
                for s2, t2 in snap.items():
                    if seen.get(s2, 0) < t2:
                        seen[s2] = t2

    def _record(self, src, t, reads, writes):
        for k in reads:
            self.readers.setdefault(k, {})[src] = t
        for k in writes:
            self.last_w[k] = (src, t)
            self.readers[k] = {}

    def op(self, e, fn, reads=(), writes=()):
        self._wait(e, self._need(reads, writes))
        ins = fn(self.eng[e])
        self.tick[e] += 1
        t = self.tick[e]
        ins.then_inc(self.sem[e], 1)
        self.clocks[(e, t)] = dict(self.seen[e])
        self._record(e, t, reads, writes)
        self.n_ops += 1
        return ins

    def dma(self, q, out, in_, reads=(), writes=(), **kw):
        if q == "pool":
            i = self.dnext_sw
            self.dnext_sw = (self.dnext_sw + 1) % N_SW_SEMS
        else:
            i = N_SW_SEMS + self.dnext_hw
            self.dnext_hw = (self.dnext_hw + 1) % (N_DMA_SEMS - N_SW_SEMS)
        need = self._need(reads, writes)
        if self.dtick[i] > 0 and need.get(i, 0) < self.dtick[i]:
            need[i] = self.dtick[i]
        self._wait(q, need)
        ins = self.eng[q].dma_start(out=out, in_=in_, **kw)
        self.dtick[i] += 16
        t = self.dtick[i]
        ins.then_inc(self.dsem[i], 16)
        self.clocks[(i, t)] = dict(self.seen[q])
        self._record(i, t, reads, writes)
        return ins

    def wait_all(self, e):
        need = {s: self.tick[s] for s in ENGS if self.tick[s] > 0}
        for i in range(N_DMA_SEMS):
            if self.dtick[i] > 0:
                need[i] = self.dtick[i]
        self._wait(e, need)


def bcast_mid(ap2d, n):
    return ap2d.unsqueeze(1).broadcast_to([ap2d.shape[0], n, ap2d.shape[1]])


class Prog:
    def __init__(self, stop_after=None, start_layer=0):
        self.stop_after = stop_after
        self.start_layer = start_layer
        self.na_norm_eng = "dve"
        self.nc = bass.Bass("TRN2", target_bir_lowering=False)
        nc = self.nc
        dt = nc.dram_tensor
        self.d = {}

        def din(name, shape):
            self.d[name] = dt(name, list(shape), F32, kind="ExternalInput").ap()

        din("xin", [D, NT])
        din("cc", [128, NCH, 2])
        din("modw", [DEPTH, 36, 128, 8 * 256])
        din("modb", [128, DEPTH, 72])
        din("lng", [128, DEPTH, 3, NCH])
        din("lnb", [128, DEPTH, 3, NCH])
        din("ffw1", [DEPTH, 2, NJ, 128, 8 * 256])
        din("ffw2", [DEPTH, 2, 2, NCH, 128, NJH * 128])
        din("ident", [128, 128])
        din("scw1", [NCH, 3, 128, 8 * 128])
        din("scw2", [NCH, 128, 8 * 128])
        din("sccw", [128, 3, NCH])
        din("lruw1", [NBLK, 128, 8 * 2 * BW])
        din("lrug", [NBLK, BW, 4 * BW])
        din("lruw2", [NBLK, BW, D])
        din("lruvec", [BW, NBLK, 11])
        din("naw1", [2, NCH, 3, 128, 8 * 128])
        din("naw2", [2, NCH, 128, 8 * 128])
        din("rpbt", [2, HEADS, 64, 15 * 64])
        din("cmask", [128, 64])
        self.out = dt("out", [D, NT], F32, kind="ExternalOutput").ap()

    def build(self):
        nc = self.nc
        with ExitStack() as st:
            self.st = st
            self.k = K(nc, st)
            sb = lambda name, shape, dtype: st.enter_context(nc.sbuf_tensor(name, list(shape), dtype))
            self.X = sb("X", [128, NCH, NT], F32)
            self.H = sb("H", [128, NCH, NT], BF16)
            self.A = sb("A", [128, NJH * NT], BF16)
            self.WR = sb("WR", [128, RING_SLOTS, SLOT_ELEMS], BF16)
            self.SCR = sb("SCR", [128, 2304], F32)
            self.LNT = sb("LNT", [128, 4 * 512], F32)
            self.T1 = self.LNT[:, 0:512]
            self.T2 = self.LNT[:, 512:1024]
            self.RS = self.LNT[:, 1024:1536]
            self.NM = self.LNT[:, 1536:2048]
            self.STAT = sb("STAT", [128, 32], F32)
            self.CMASK = sb("CMASK", [128, 64], F32)
            self.LV = sb("LV", [BW, NBLK, 24], F32)
            self.M = sb("M", [128, 72, 2], F32)
            self.VEC = sb("VEC", [128, 36, NCH, 2], F32)
            self.LNG = sb("LNG", [128, DEPTH, 3, NCH], F32)
            self.LNB = sb("LNB", [128, DEPTH, 3, NCH], F32)
            self.MODB = sb("MODB", [128, DEPTH, 72], F32)
            self.CC = sb("CC", [128, NCH, 2], F32)
            self.SC16 = sb("SC16", [128, NCH, 2], BF16)
            self.IDF = self.LNT[:, 0:128]
            self.IDB = sb("IDB", [128, 128], BF16)
            self.ONESD = sb("ONESD", [128, 128], BF16)
            self.S1N = sb("S1N", [128, NCH, 2], F32)
            self.EPS = sb("EPS", [128, 1], F32)
            self.CONST = sb("CONST", [128, 4], F32)
            self.PS2 = [st.enter_context(nc.psum_tensor("ps%d" % i, [128, 1024], F32)) for i in range(4)]
            self.PS = [self.PS2[i // 2][:, (i % 2) * 512:(i % 2 + 1) * 512] for i in range(8)]
            self.bg = []
            self.ring_limit = RING_SLOTS
            self.ring_next = 0
            self.vec_next = 0
            self.vidx = {}
            self._body()
            self.k.wait_all("sp")
        return nc

    def wload(self, dram_ap, n_elems, parts=128):
        assert n_elems <= SLOT_ELEMS
        s = self.ring_next
        self.ring_next = (s + 1) % self.ring_limit
        dst = self.WR[0:parts, s, 0:n_elems]
        self.k.dma("pool", dst, dram_ap, writes=[("W", s)])
        return s, dst

    def bg_step(self, n=1):
        for _ in range(n):
            if self.bg:
                self.bg.pop(0)()

    def bg_flush(self):
        while self.bg:
            self.bg.pop(0)()

    def valloc(self, name):
        i = self.vec_next
        self.vec_next += 1
        self.vidx[name] = i
        return i

    def V(self, name, c, s):
        return self.VEC[:, self.vidx[name], c, s:s + 1]

    def Vfull(self, name):
        return self.VEC[:, self.vidx[name], :, :]

    def atail(self):
        base = NJH * NT - 2 * NCH * 512
        xb = self.A[:, base:base + NCH * 512].rearrange("p (c t) -> p c t", c=NCH)
        xq = self.A[:, base + NCH * 512:base + 2 * NCH * 512].rearrange("p (c t) -> p c t", c=NCH)
        keys = [("A", jj, ti) for jj in range(7, NJH) for ti in range(5)]
        return xb, xq, keys

    def Aview(self, jj):
        return self.A[:, jj * NT:(jj + 1) * NT]

    def _body(self):
        k = self.k
        d = self.d
        k.dma("sp", self.CC[:], d["cc"], writes=["CC"])
        k.dma("sp", self.MODB[:], d["modb"], writes=["MODB"])
        k.dma("sp", self.LNG[:], d["lng"], writes=["LNG"])
        k.dma("sp", self.LNB[:], d["lnb"], writes=["LNB"])
        k.dma("sp", self.IDF, d["ident"], writes=["T1"])
        k.op("dve", lambda e: e.tensor_copy(out=self.IDB[:], in_=self.IDF), reads=["T1"], writes=["IDB"])
        k.op("dve", lambda e: e.memset(self.ONESD[:], 1.0 / D), writes=["ONESD"])
        k.op("dve", lambda e: e.memset(self.EPS[:], EPS_P), writes=["EPS"])
        k.op("dve", lambda e: e.memset(self.CONST[:, 0:1], 1.0), writes=["CONST"])
        k.op("dve", lambda e: e.memset(self.CONST[:, 1:2], 0.25), writes=["CONST"])
        k.op("dve", lambda e: e.memset(self.CONST[:, 2:3], 0.0), writes=["CONST"])
        xv = d["xin"].rearrange("(c p) t -> p c t", p=128)
        for (ti, t0, tn, s) in TILES:
            k.dma("sp", self.X[:, :, t0:t0 + tn], xv[:, :, t0:t0 + tn], writes=[("X", c, ti) for c in range(NCH)])
        k.op("act", lambda e: e.activation(out=self.SC16[:], in_=self.CC[:], func=AF.Silu), reads=["CC"], writes=["SC16"])
        first_jobs = self.mod_jobs(self.start_layer)
        for jb in first_jobs[:12]:
            jb()
        self.mod_vectors(self.start_layer, part=0)
        self.bg.extend(first_jobs[12:])
        self._first_mod_pending = True
        for (ti, t0, tn, s) in TILES:
            for c in range(NCH):
                k.op("dve", lambda e, c=c, t0=t0, tn=tn, s=s: e.tensor_scalar(
                    out=self.H[:, c, t0:t0 + tn], in0=self.X[:, c, t0:t0 + tn],
                    scalar1=self.V("HS_init", c, s), scalar2=self.V("HB_init", c, s), op0=ALU.mult, op1=ALU.add),
                    reads=[("X", c, ti), "VEC"], writes=[("H", ti)])
        done = False
        for l in range(self.start_layer, DEPTH):
            last = (l == DEPTH - 1)
            if not last and l != self.start_layer:
                self.bg.extend(self.mod_jobs(l + 1))
            for j in range(3):
                tiles = TILES if not (last and j >= 1) else TILES[:4]
                if j == 1:
                    self.mixer(l, tiles)
                else:
                    self.ffn(l, j // 2, j, tiles)
                if j == 0 and l == self.start_layer:
                    self.bg_flush()
                    self.mod_vectors(l, part=1)
                elif j == 0 and not last:
                    self.bg_flush()
                    self.mod_vectors(l + 1)
                if j == 1 and l == self.start_layer and not last:
                    self.bg.extend(self.mod_jobs(l + 1))
                if j == 2 and l == self.start_layer and not last:
                    self.bg_flush()
                    self.mod_vectors(l + 1)
                self.post_norm(l, j, tiles, make_h=not (last and j == 2))
                if self.stop_after == (l, j):
                    done = True
                    break
            if done:
                break
        ov = self.out.rearrange("(c p) t -> p c t", p=128)
        for (ti, t0, tn, s) in TILES:
            k.dma("sp", ov[:, :, t0:t0 + tn], self.X[:, :, t0:t0 + tn],
                  reads=[("X", c, ti) for c in range(NCH)], writes=[("OUT", ti)])

    def mod_jobs(self, l):
        k = self.k
        psm = self.PS[6]

        def job(s36):
            def run():
                slot, w = self.wload(self.d["modw"][l, s36], 8 * 256)
                wv = w.rearrange("p (kc n) -> p kc n", kc=8)

                def mm(e):
                    for q2 in range(2):
                        for kc in range(8):
                            ins = e.matmul(psm[:, 2 * q2:2 * q2 + 2], lhsT=wv[:, kc, q2 * 128:(q2 + 1) * 128],
                                           rhs=self.SC16[:, kc, :], start=(kc == 0), stop=(kc == 7))
                    return ins
                k.op("pe", mm, reads=[("W", slot), "SC16"], writes=[("PS", 6)])
                mb = self.MODB[:, l, 2 * s36:2 * s36 + 2].unsqueeze(2).broadcast_to([128, 2, 2])
                k.op("dve", lambda e: e.tensor_tensor(out=self.M[:, 2 * s36:2 * s36 + 2, :], in0=psm[:, 0:4].rearrange("p (q s) -> p q s", s=2),
                                                      in1=mb, op=ALU.add),
                     reads=["MODB"], writes=[("PS", 6), "M"])
            return run
        return [job(i) for i in range(36)]

    def mod_vectors(self, l, part=None):
        k = self.k

        def msl(jm):
            return self.M[:, jm * 8:(jm + 1) * 8, :]

        def hs_hb(name_l, name_j, nj):
            ihs = self.valloc(("HS", name_l, name_j))
            ihb = self.valloc(("HB", name_l, name_j))
            k.op("dve", lambda e: e.tensor_scalar(out=self.S1N[:], in0=msl(3 * nj + 1), scalar1=1.0, scalar2=None, op0=ALU.add),
                 reads=["M"], writes=["S1N"])
            lg = self.LNG[:, name_l, name_j, :].unsqueeze(2).broadcast_to([128, NCH, 2])
            lb = self.LNB[:, name_l, name_j, :].unsqueeze(2).broadcast_to([128, NCH, 2])
            k.op("dve", lambda e: e.tensor_tensor(out=self.VEC[:, ihs], in0=self.S1N[:], in1=lg, op=ALU.mult),
                 reads=["S1N", "LNG"], writes=["VEC"])
            k.op("dve", lambda e: e.tensor_tensor(out=self.VEC[:, ihb], in0=self.S1N[:], in1=lb, op=ALU.mult),
                 reads=["S1N", "LNB"], writes=["VEC"])
            k.op("dve", lambda e: e.tensor_tensor(out=self.VEC[:, ihb], in0=self.VEC[:, ihb], in1=msl(3 * nj), op=ALU.add),
                 reads=["VEC", "M"], writes=["VEC"])
        if part == 1:
            for j in (1, 2):
                fac = (0.5 if j != 1 else 1.0) / ALPHA
                ig = self.valloc(("GP", l, j))
                k.op("dve", lambda e: e.tensor_scalar(out=self.VEC[:, ig], in0=msl(3 * j + 2), scalar1=fac, scalar2=None, op0=ALU.mult),
                     reads=["M"], writes=["VEC"])
            hs_hb(l, 0, 1)
            hs_hb(l, 1, 2)
            return
        if l == self.start_layer:
            i0 = self.valloc("HS_init")
            i1 = self.valloc("HB_init")
            k.op("dve", lambda e: e.tensor_scalar(out=self.VEC[:, i0], in0=msl(1), scalar1=1.0, scalar2=None, op0=ALU.add),
                 reads=["M"], writes=["VEC"])
            k.op("dve", lambda e: e.tensor_copy(out=self.VEC[:, i1], in_=msl(0)), reads=["M"], writes=["VEC"])
        else:
            hs_hb(l - 1, 2, 0)
        for j in range(3):
            if part == 0 and j > 0:
                break
            fac = (0.5 if j != 1 else 1.0) / ALPHA
            ig = self.valloc(("GP", l, j))
            k.op("dve", lambda e: e.tensor_scalar(out=self.VEC[:, ig], in0=msl(3 * j + 2), scalar1=fac, scalar2=None, op0=ALU.mult),
                 reads=["M"], writes=["VEC"])
        if part == 0:
            return
        hs_hb(l, 0, 1)
        hs_hb(l, 1, 2)

    def ffn(self, l, f, j, tiles):
        k = self.k
        PS = self.PS
        G0 = 4
        self._ffn_it = 0

        def phase_a(jj, slot, wv, ti, t0, tn):
            it = self._ffn_it
            self._ffn_it += 1
            gb = it % 2
            ub = 2 + it % 2
            sgb = it % 2

            def mm(e):
                for kc in range(8):
                    e.matmul(PS[gb][:, :tn], lhsT=wv[:, kc, 0:128], rhs=self.H[:, kc, t0:t0 + tn], start=(kc == 0), stop=(kc == 7))
                for kc in range(8):
                    ins = e.matmul(PS[ub][:, :tn], lhsT=wv[:, kc, 128:256], rhs=self.H[:, kc, t0:t0 + tn], start=(kc == 0), stop=(kc == 7))
                return ins
            k.op("pe", mm, reads=[("W", slot), ("H", ti)], writes=[("PS", gb), ("PS", ub)])
            sg = self.SCR[:, sgb * 512:sgb * 512 + tn]
            k.op("act", lambda e: e.activation(out=sg, in_=PS[gb][:, :tn], func=AF.Silu), writes=[("PS", gb), ("SCR", sgb)])
            k.op("dve", lambda e: e.tensor_tensor(out=self.Aview(jj)[:, t0:t0 + tn], in0=sg, in1=PS[ub][:, :tn], op=ALU.mult),
                 reads=[("SCR", sgb)], writes=[("PS", ub), ("A", jj, ti)])

        def load1(jg):
            slot, w = self.wload(self.d["ffw1"][l, f, jg], 8 * 256)
            return slot, w.rearrange("p (kc n) -> p kc n", kc=8)

        for hf in range(2):
            jj0 = 0
            if hf == 0:
                grp = [load1(jj) for jj in range(G0)]
                for (ti, t0, tn, s) in tiles:
                    for jj in range(G0):
                        phase_a(jj, grp[jj][0], grp[jj][1], ti, t0, tn)
                for _ in range(G0):
                    self.bg_step()
                jj0 = G0
            for jj in range(jj0, NJH):
                slot, wv = load1(hf * NJH + jj)
                for (ti, t0, tn, s) in tiles:
                    phase_a(jj, slot, wv, ti, t0, tn)
                self.bg_step()
            for dc in range(NCH):
                slot, w = self.wload(self.d["ffw2"][l, f, hf, dc], NJH * 128)
                wv = w.rearrange("p (jj n) -> p jj n", jj=NJH)
                for (ti, t0, tn, s) in tiles:
                    yb = 4 + self._ffn_it % 2
                    self._ffn_it += 1

                    def mm2(e):
                        for jj in range(NJH):
                            ins = e.matmul(PS[yb][:, :tn], lhsT=wv[:, jj, :], rhs=self.Aview(jj)[:, t0:t0 + tn],
                                           start=(jj == 0), stop=(jj == NJH - 1))
                        return ins
                    k.op("pe", mm2, reads=[("W", slot)] + [("A", jj, ti) for jj in range(NJH)], writes=[("PS", yb)])
                    self.x_update(l, j, dc, ti, t0, tn, s, yb)
                self.bg_step()

    def x_update(self, l, j, dc, ti, t0, tn, s, yb):
        gp = self.V(("GP", l, j), dc, s)
        self.k.op("dve", lambda e: e.scalar_tensor_tensor(
            out=self.X[:, dc, t0:t0 + tn], in0=self.PS[yb][:, :tn], scalar=gp, in1=self.X[:, dc, t0:t0 + tn],
            op0=ALU.mult, op1=ALU.add),
            reads=["VEC"], writes=[("PS", yb), ("X", dc, ti)])

    def post_norm(self, l, j, tiles, make_h=True):
        k = self.k
        PS = self.PS
        xb, xq, akeys = self.atail()
        n = len(tiles)

        def banks(i):
            return (6, 7) if i % 2 == 0 else (4, 5)

        def stA(i):
            (ti, t0, tn, s) = tiles[i]
            xkeys = [("X", c, ti) for c in range(NCH)]
            xs = self.X[:, :, t0:t0 + tn]
            bm, bq = banks(i)
            k.op("act", lambda e: e.activation(out=xb[:, :, :tn], in_=xs, func=AF.Copy), reads=xkeys, writes=akeys)
            k.op("act", lambda e: e.activation(out=xq[:, :, :tn], in_=xs, func=AF.Square), reads=xkeys, writes=akeys)

            def mm(e):
                for c in range(NCH):
                    e.matmul(PS[bm][:, :tn], lhsT=self.ONESD[:], rhs=xb[:, c, :tn], start=(c == 0), stop=(c == NCH - 1))
                for c in range(NCH):
                    ins = e.matmul(PS[bq][:, :tn], lhsT=self.ONESD[:], rhs=xq[:, c, :tn], start=(c == 0), stop=(c == NCH - 1))
                return ins
            k.op("pe", mm, reads=akeys + ["ONESD"], writes=[("PS", bm), ("PS", bq)])

        def stBC(i):
            (ti, t0, tn, s) = tiles[i]
            xkeys = [("X", c, ti) for c in range(NCH)]
            xs = self.X[:, :, t0:t0 + tn]
            bm, bq = banks(i)
            T1, RS, NM = self.T1[:, :tn], self.RS[:, :tn], self.NM[:, :tn]
            k.op("act", lambda e: e.activation(out=T1, in_=PS[bm][:, :tn], func=AF.Square), writes=[("PS", bm), "T1"])
            k.op("dve", lambda e: e.tensor_tensor(out=T1, in0=PS[bq][:, :tn], in1=T1, op=ALU.subtract), writes=[("PS", bq), "T1"])
            k.op("act", lambda e: e.activation(out=T1, in_=T1, func=AF.Ln, bias=self.EPS[:, 0:1], scale=1.0), reads=["EPS"], writes=["T1"])
            k.op("act", lambda e: e.activation(out=RS, in_=T1, func=AF.Exp, scale=-0.5), reads=["T1"], writes=["RS"])
            k.op("dve", lambda e: e.scalar_tensor_tensor(out=NM, in0=PS[bm][:, :tn], scalar=-1.0, in1=RS, op0=ALU.mult, op1=ALU.mult),
                 reads=["RS"], writes=[("PS", bm), "NM"])
            k.op("dve", lambda e: e.tensor_tensor(out=xs, in0=xs, in1=bcast_mid(RS, NCH), op=ALU.mult), reads=["RS"], writes=xkeys)
            k.op("dve", lambda e: e.tensor_tensor(out=xs, in0=xs, in1=bcast_mid(NM, NCH), op=ALU.add), reads=["NM"], writes=xkeys)

        def stC(i):
            (ti, t0, tn, s) = tiles[i]
            for c in range(NCH):
                xc = self.X[:, c, t0:t0 + tn]
                if make_h:
                    k.op("dve", lambda e: e.tensor_scalar(
                        out=self.H[:, c, t0:t0 + tn], in0=xc, scalar1=self.V(("HS", l, j), c, s),
                        scalar2=self.V(("HB", l, j), c, s), op0=ALU.mult, op1=ALU.add),
                        reads=[("X", c, ti), "VEC"], writes=[("H", ti)])
                k.op("act", lambda e: e.activation(out=xc, in_=xc, func=AF.Identity,
                                                   bias=self.LNB[:, l, j, c:c + 1], scale=self.LNG[:, l, j, c:c + 1]),
                     reads=["LNG", "LNB"], writes=[("X", c, ti)])

        stA(0)
        for i in range(n):
            stBC(i)
            if i + 1 < n:
                stA(i + 1)
            stC(i)


    def mixer(self, l, tiles):
        kind = l % 3
        if kind == 0:
            self.na_mixer(l, l // 3, tiles)
        elif kind == 1:
            self.lru_mixer(l, tiles)
        else:
            self.sc_mixer(l, tiles)

    def out_proj(self, l, wdram, src, src_keys, tiles):
        k = self.k
        PS = self.PS
        it = 0
        for dc in range(NCH):
            slot, w = self.wload(wdram[dc], 8 * 128)
            wv = w.rearrange("p (kc n) -> p kc n", kc=8)
            for (ti, t0, tn, s) in tiles:
                yb = 4 + it % 2
                it += 1

                def mm(e):
                    for kc in range(8):
                        ins = e.matmul(PS[yb][:, :tn], lhsT=wv[:, kc, :], rhs=src(kc)[:, t0:t0 + tn],
                                       start=(kc == 0), stop=(kc == 7))
                    return ins
                k.op("pe", mm, reads=[("W", slot)] + src_keys(ti), writes=[("PS", yb)])
                self.x_update(l, 1, dc, ti, t0, tn, s, yb)


    def fence(self, keys):
        self.k.op("dve", lambda e: e.memset(self.STAT[:, 31:32], 0.0), writes=list(keys) + ["FENCE"])

    def na_mixer(self, l, idx, tiles):
        k = self.k
        PS = self.PS
        PS2 = self.PS2
        d = self.d
        need_ctx = (len(tiles) == 5)
        scale = HD ** -0.5
        WRf = self.WR[:, 2:6, :].rearrange("p s n -> p (s n)")
        bufs = [
            dict(QT=self.Aview(8), KT=self.Aview(9), VT=self.Aview(10).rearrange("p (tk n) -> p tk n", tk=18),
                 kQ=[("A", 8, t) for t in range(5)], kK=[("A", 9, t) for t in range(5)], kV=[("A", 10, t) for t in range(5)]),
            dict(QT=WRf[:, 0:NT], KT=WRf[:, NT:2 * NT], VT=WRf[:, 2 * NT:3 * NT].rearrange("p (tk n) -> p tk n", tk=18),
                 kQ=[("W", 2), ("W", 3)], kK=[("W", 3), ("W", 4)], kV=[("W", 4), ("W", 5)]),
        ]
        SB16 = self.SCR[:].bitcast(BF16)
        L16 = self.LNT[:].bitcast(BF16)
        NPB = 4
        PB = [SB16[:, b * 896:(b + 1) * 896] for b in range(NPB)]
        EW = SB16[:, 3584:3584 + 960].rearrange("p (r c) -> p r c", r=15)
        PTS = [L16[:, 1920 + b * 896:1920 + (b + 1) * 896].rearrange("p (g q) -> p g q", g=7) for b in range(2)]
        EB = self.LNT[:, 0:960].rearrange("p (r c) -> p r c", r=15)
        wkeys = [("W", i) for i in range(RING_SLOTS)]
        guard = [("SCR", i) for i in range(5)] + ["T1", "T2", "RS", "NM"] + wkeys
        na_keys = [("NA", "PB", i) for i in range(NPB)] + [("NA", "PTS", 0), ("NA", "PTS", 1), ("NA", "EW")]
        self.fence(guard + na_keys)
        self.ring_limit = 2
        self.ring_next = 0
        for b in range(NPB):
            k.op("dve", lambda e: e.memset(PB[b][:, 0:64], 0.0), writes=[("NA", "PB", b)])
            k.op("dve", lambda e: e.memset(PB[b][:, 576:640], 0.0), writes=[("NA", "PB", b)])
        k.dma("sp", self.CMASK[:], d["cmask"], writes=["CMASK"])
        rp = d["rpbt"]
        groups = [("lat", r) for r in range(ROWS)] + ([("ctx", g) for g in range(4)] if need_ctx else [])
        NG = len(groups)
        gcount = 0
        PJ = 7

        def proj_jobs(pr):
            B = bufs[pr % 2]
            jobs = []
            st = {}

            def ld(which):
                def run():
                    slot, w = self.wload(d["naw1"][idx, pr, which], 1024)
                    st[which] = (slot, w.rearrange("p (kc n) -> p kc n", kc=8))
                return run
            for (which, dst, dkeys, tl) in ((0, B["QT"], B["kQ"], tiles), (1, B["KT"], B["kK"], TILES)):
                jobs.append(ld(which))
                for (ti, t0, tn, s_) in tl:
                    def run(which=which, dst=dst, dkeys=dkeys, ti=ti, t0=t0, tn=tn):
                        wslot, wmat = st[which]

                        def mm(e):
                            for kc in range(8):
                                ins = e.matmul(PS[PJ][:, :tn], lhsT=wmat[:, kc, :], rhs=self.H[:, kc, t0:t0 + tn], start=(kc == 0), stop=(kc == 7))
                            return ins
                        k.op("pe", mm, reads=[("W", wslot), ("H", ti)], writes=[("PS", PJ)])
                        if which == 0:
                            k.op("act", lambda e: e.activation(out=dst[:, t0:t0 + tn], in_=PS[PJ][:, :tn], func=AF.Copy, scale=scale),
                                 writes=[("PS", PJ)] + dkeys)
                        else:
                            k.op("dve", lambda e: e.tensor_copy(out=dst[:, t0:t0 + tn], in_=PS[PJ][:, :tn]), writes=[("PS", PJ)] + dkeys)
                    jobs.append(run)
            jobs.append(ld(2))
            for g in range(5):
                def runv(g=g):
                    sv, wv = st[2]
                    ntk = 4 if g < 4 else 2

                    def mmv(e):
                        for q in range(ntk):
                            tk = 4 * g + q
                            for kc in range(8):
                                ins = e.matmul(PS[PJ][:, q * 128:(q + 1) * 128], lhsT=self.H[:, kc, tk * 128:(tk + 1) * 128], rhs=wv[:, kc, :],
                                               start=(kc == 0), stop=(kc == 7))
                        return ins
                    k.op("pe", mmv, reads=[("W", sv), ("H", g)], writes=[("PS", PJ)])
                    k.op("act", lambda e: e.activation(out=B["VT"][:, 4 * g:4 * g + ntk, :],
                                                       in_=PS[PJ][:, 0:ntk * 128].rearrange("p (q n) -> p q n", q=ntk), func=AF.Copy),
                         writes=[("PS", PJ)] + B["kV"])
                jobs.append(runv)
            return jobs

        j0 = proj_jobs(0)
        nq = len(tiles)
        lds = [j0[0], j0[1 + nq], j0[2 + nq + 5]]
        qj = j0[1:1 + nq]
        kj = j0[2 + nq:2 + nq + 5]
        vj = j0[3 + nq + 5:]
        self.ring_limit = 3
        for jb in lds:
            jb()
        self.ring_limit = 2
        self.ring_next = 0
        for i in range(5):
            if i < nq:
                qj[i]()
            kj[i]()
            vj[i]()
        for pr in range(NCH):
            B = bufs[pr % 2]
            QT, KT, VT, kQ, kK, kV = B["QT"], B["KT"], B["VT"], B["kQ"], B["kK"], B["kV"]
            for par in range(2):
                h = 2 * pr + par
                k.dma("sp", self.LNT[par * 64:(par + 1) * 64, 0:960], rp[idx, h], writes=["T1", "T2"])
            k.op("act", lambda e: e.activation(out=EB, in_=EB, func=AF.Exp), reads=[], writes=["T1", "T2"])
            k.op("dve", lambda e: e.tensor_tensor(out=EW, in0=EB, in1=self.CMASK[:].unsqueeze(1).broadcast_to([128, 15, 64]), op=ALU.mult),
                 reads=["T1", "T2", "CMASK"], writes=[("NA", "EW")])
            if pr + 1 < NCH:
                self.bg.extend(proj_jobs(pr + 1))
            info = []
            for gi, (kind, r) in enumerate(groups):
                gg = gcount + gi
                lat = (kind == "lat")
                r0 = min(max(r - WIN_R // 2, 0), ROWS - WIN_R) if lat else 0
                qc0 = r * GRID_W if lat else TL + r * 64
                if lat:
                    if r0 % 2 == 0:
                        blocks = [(64 + 128 * i, r0 // 2 + i) for i in range(4)]
                    else:
                        blocks = [(128 * i, (r0 - 1) // 2 + i) for i in range(5)]
                else:
                    blocks = []
                blocks = blocks + [(640, 16), (768, 17)]
                info.append(dict(gi=gi, lat=lat, r=r, r0=r0, qc0=qc0, blocks=blocks, sb=gg % 2, pbi=gg % NPB, sti=gg % 4))
            gcount += NG

            def st1(g):
                S = PS2[g["sb"]]
                kS = [("PS", 2 * g["sb"]), ("PS", 2 * g["sb"] + 1)]
                st_ = self.STAT[:, g["sti"] * 8:g["sti"] * 8 + 8]
                qc0, r0 = g["qc0"], g["r0"]

                def mms(e):
                    for par in range(2):
                        ps_ = slice(par * 64, (par + 1) * 64)
                        if g["lat"]:
                            e.matmul(S[ps_, 0:512], lhsT=QT[ps_, qc0:qc0 + 64], rhs=KT[ps_, r0 * 64:r0 * 64 + 512], start=True, stop=True)
                        ins = e.matmul(S[ps_, 512:768], lhsT=QT[ps_, qc0:qc0 + 64], rhs=KT[ps_, TL:NT], start=True, stop=True)
                    return ins
                k.op("pe", mms, reads=kQ + kK, writes=kS)
                lo = 0 if g["lat"] else 512
                k.op("dve", lambda e: e.reduce_max(out=st_[:, 0:1], in_=S[:, lo:768], axis=mybir.AxisListType.X, negate=True),
                     writes=kS + [("ST", g["sti"])])

            def st2(g):
                S = PS2[g["sb"]]
                kS = [("PS", 2 * g["sb"]), ("PS", 2 * g["sb"] + 1)]
                st_ = self.STAT[:, g["sti"] * 8:g["sti"] * 8 + 8]
                pbuf = PB[g["pbi"]]
                kpb = ("NA", "PB", g["pbi"])
                if g["lat"]:
                    k.op("act", lambda e: e.activation(out=pbuf[:, 64:576], in_=S[:, 0:512], func=AF.Exp, bias=st_[:, 0:1], scale=1.0),
                         reads=[("ST", g["sti"])], writes=kS + [kpb])
                k.op("act", lambda e: e.activation(out=pbuf[:, 640:896], in_=S[:, 512:768], func=AF.Exp, bias=st_[:, 0:1], scale=1.0,
                                                   accum_out=st_[:, 1:2]),
                     reads=[], writes=kS + [kpb, ("ST", g["sti"])])

            def st3(g):
                st_ = self.STAT[:, g["sti"] * 8:g["sti"] * 8 + 8]
                kst = ("ST", g["sti"])
                pbuf = PB[g["pbi"]]
                kpb = ("NA", "PB", g["pbi"])
                if g["lat"]:
                    d0 = g["r0"] - g["r"] + 7
                    pw = pbuf[:, 64:576].rearrange("p (r c) -> p r c", r=8)
                    k.op("dve", lambda e: e.scalar_tensor_tensor(out=pw, in0=pw, scalar=1.0, in1=EW[:, d0:d0 + 8, :], op0=ALU.mult, op1=ALU.mult,
                                                                 accum_out=st_[:, 2:3]),
                         reads=[("NA", "EW")], writes=[kpb, kst])
                    k.op("dve", lambda e: e.tensor_tensor(out=st_[:, 3:4], in0=st_[:, 1:2], in1=st_[:, 2:3], op=ALU.add), writes=[kst])
                    k.op("dve", lambda e: e.reciprocal(out=st_[:, 4:5], in_=st_[:, 3:4]), writes=[kst])
                else:
                    k.op("dve", lambda e: e.reciprocal(out=st_[:, 4:5], in_=st_[:, 1:2]), writes=[kst])

            def st3b(g):
                st_ = self.STAT[:, g["sti"] * 8:g["sti"] * 8 + 8]
                kst = ("ST", g["sti"])
                pbuf = PB[g["pbi"]]
                kpb = ("NA", "PB", g["pbi"])
                lo2 = 64 if g["lat"] else 640
                k.op(self.na_norm_eng, lambda e: e.tensor_scalar(out=pbuf[:, lo2:896], in0=pbuf[:, lo2:896], scalar1=st_[:, 4:5], scalar2=None,
                                                                 op0=ALU.mult), reads=[kst], writes=[kpb])

            def st4(g):
                b = g["sb"]
                PTv = PS[4 + b].bitcast(BF16)
                pbuf = PB[g["pbi"]]

                def mmt(e):
                    for bi, (col, tk) in enumerate(g["blocks"]):
                        ins = e.transpose(out=PTv[:, bi * 128:(bi + 1) * 128], in_=pbuf[:, col:col + 128], identity=self.IDB[:])
                    return ins
                k.op("pe", mmt, reads=[("NA", "PB", g["pbi"]), "IDB"], writes=[("PS", 4 + b)])

            def st5(g):
                b = g["sb"]
                PTv = PS[4 + b].bitcast(BF16)
                nblk = len(g["blocks"])
                k.op("act", lambda e: e.activation(out=PTS[b][:, 0:nblk, :], in_=PTv[:, 0:nblk * 128].rearrange("p (g q) -> p g q", g=nblk),
                                                   func=AF.Copy),
                     writes=[("PS", 4 + b), ("NA", "PTS", b)])

            def st6(g):
                b = g["sb"]
                gi = g["gi"]
                blk8 = gi // 8
                ob = 6
                oc = (gi % 8) * 64
                nblk = len(g["blocks"])

                def mmo(e):
                    for par in range(2):
                        ps_ = slice(par * 64, (par + 1) * 64)
                        for bi, (col, tk) in enumerate(g["blocks"]):
                            ins = e.matmul(PS[ob][ps_, oc:oc + 64], lhsT=VT[:, tk, ps_], rhs=PTS[b][:, bi, ps_],
                                           start=(bi == 0), stop=(bi == nblk - 1))
                    return ins
                k.op("pe", mmo, reads=kV + [("NA", "PTS", b)], writes=[("PS", ob)])
                if (gi % 8 == 7) or (gi == NG - 1):
                    ncols = oc + 64
                    tok0 = blk8 * 512
                    k.op("act", lambda e: e.activation(out=self.Aview(pr)[:, tok0:tok0 + ncols], in_=PS[ob][:, 0:ncols], func=AF.Copy),
                         writes=[("PS", ob), ("A", pr, blk8)])

            stages = [st1, st2, st3, st3b, st4, st5, st6]
            for t in range(NG + len(stages) - 1):
                for si, fn in enumerate(stages):
                    gi = t - si
                    if 0 <= gi < NG:
                        fn(info[gi])
                self.bg_step()
            self.bg_flush()
        self.fence(guard + na_keys)
        self.ring_limit = RING_SLOTS
        self.ring_next = 0
        self.out_proj(l, d["naw2"][idx], lambda kc: self.Aview(kc), lambda ti: [("A", kc, ti) for kc in range(8)], tiles)

    def lru_prep(self):
        k = self.k
        LV = self.LV
        k.dma("sp", LV[:, :, 0:11], self.d["lruvec"], writes=["LV"])
        lam = LV[:, :, 9:11]
        e = LV[:, :, 19:21]
        p = LV[:, :, 21:23]
        sp = LV[:, :, 15:17]
        rw = dict(reads=["LV", "CONST"], writes=["LV"])
        k.op("act", lambda a: a.activation(out=e, in_=lam, func=AF.Exp, scale=-1.0), **rw)
        k.op("dve", lambda v: v.tensor_scalar(out=p, in0=e, scalar1=-1.0 / 6.0, scalar2=None, op0=ALU.mult), **rw)
        for cst in (1.0 / 5.0, -1.0 / 4.0, 1.0 / 3.0, -1.0 / 2.0, 1.0):
            k.op("dve", lambda v, cst=cst: v.scalar_tensor_tensor(out=p, in0=p, scalar=cst, in1=e, op0=ALU.add, op1=ALU.mult), **rw)
        k.op("act", lambda a: a.activation(out=sp, in_=e, func=AF.Ln, bias=self.CONST[0:BW, 0:1], scale=1.0), **rw)
        k.op("dve", lambda v: v.tensor_tensor(out=p, in0=p, in1=sp, op=ALU.subtract), **rw)
        k.op("dve", lambda v: v.tensor_single_scalar(out=e, in_=e, scalar=0.1, op=ALU.is_lt), **rw)
        k.op("dve", lambda v: v.tensor_tensor(out=p, in0=p, in1=e, op=ALU.mult), **rw)
        k.op("dve", lambda v: v.tensor_tensor(out=sp, in0=sp, in1=p, op=ALU.add), **rw)
        k.op("dve", lambda v: v.tensor_scalar(out=LV[:, :, 17:19], in0=sp, scalar1=-8.0, scalar2=None, op0=ALU.mult), **rw)
        k.op("dve", lambda v: v.tensor_scalar(out=LV[:, :, 15:17], in0=sp, scalar1=-4.0, scalar2=None, op0=ALU.mult), **rw)
        k.op("dve", lambda v: v.tensor_scalar(out=LV[:, :, 11:15], in0=LV[:, :, 5:9], scalar1=0.5, scalar2=None, op0=ALU.mult), **rw)

    def lru_mixer(self, l, tiles):
        k = self.k
        PS = self.PS
        d = self.d
        LV = self.LV
        P = BW
        self.lru_prep()
        ntok = NT if len(tiles) == 5 else TL
        has_ctx = (len(tiles) == 5)
        seqs = [(0, TL)] + ([(TL, NT)] if has_ctx else [])
        NTI = len(tiles)

        def arr(i):
            return self.A[0:P, i * 2 * NT:(i + 1) * 2 * NT].bitcast(F32)

        def kt(i, ti):
            return ("LA", i, ti)

        def kall(i):
            return [("LA", i, t) for t in range(5)]
        U32, AA, E1, E2, E3 = [arr(i) for i in range(5)]
        iU32, iAA, iE1, iE2, iE3, iU16 = 0, 1, 2, 3, 4, 5
        U16 = self.A[0:P, 10 * NT:11 * NT]
        Y16 = self.SCR[:].bitcast(BF16)
        T1 = self.LNT[0:P, 0:512]
        T2 = self.LNT[0:P, 512:1024]
        T3 = self.LNT[0:P, 1024:1536]
        T4 = self.LNT[0:P, 1536:2048]
        allA = [("A", jj, t) for jj in range(NJH) for t in range(5)]
        allL = [("LA", i, t) for i in range(6) for t in range(5)] + [("LY", h_, t) for h_ in range(2) for t in range(5)]
        guard = allA + [("SCR", i) for i in range(5)] + ["T1", "T2", "RS", "NM"]
        self.fence(guard + allL)
        fwd_order = ([tiles[4]] if has_ctx else []) + list(tiles[:4])
        bwd_order = ([tiles[4]] if has_ctx else []) + list(reversed(tiles[:4]))

        def rev(ap2d, a, b):
            sl_ = ap2d[:, a:b]
            pa = sl_.ap
            return bass.AP(sl_.tensor, sl_.offset + (b - a - 1) * pa[-1][0], [list(pa[0]), [-pa[-1][0], b - a]])
        it = 0
        self._lru_it = 0
        w2slabs = {}
        blk = {}

        def lvn(n, i):
            return LV[:, n, i:i + 1]

        def steps12(n):
            nonlocal it
            def lv(i):
                return lvn(n, i)
            s1, w1 = self.wload(d["lruw1"][n], 8 * 2 * P)
            w1v = w1.rearrange("p (kc g n) -> p kc g n", kc=8, g=2)
            sg_, wg = self.wload(d["lrug"][n], 4 * P, parts=P)
            wgv = wg.rearrange("p (q n) -> p q n", q=4)

            for (ti, t0, tn, s) in tiles:
                pb = it % 2
                it += 1

                def mm(e):
                    for kc in range(8):
                        ins = e.matmul(PS[pb][0:P, :tn], lhsT=w1v[:, kc, 1, :], rhs=self.H[:, kc, t0:t0 + tn], start=(kc == 0), stop=(kc == 7))
                    return ins
                k.op("pe", mm, reads=[("W", s1), ("H", ti)], writes=[("PS", pb)])
                k.op("act", lambda e: e.activation(out=AA[:, t0:t0 + tn], in_=PS[pb][0:P, :tn], func=AF.Copy), writes=[("PS", pb), kt(iAA, ti)])
            k.op("act", lambda e: e.activation(out=U32[:, 0:ntok], in_=AA[:, 0:ntok], func=AF.Identity, scale=lv(2), bias=lv(4)),
                 reads=kall(iAA) + ["LV"], writes=kall(iU32))
            for (a, b) in seqs:
                for (wi, sh) in ((0, 2), (1, 1)):
                    k.op("dve", lambda e: e.scalar_tensor_tensor(out=U32[:, a + sh:b], in0=AA[:, a:b - sh], scalar=lv(wi), in1=U32[:, a + sh:b],
                                                                 op0=ALU.mult, op1=ALU.add), reads=kall(iAA) + ["LV"], writes=kall(iU32))
                k.op("dve", lambda e: e.scalar_tensor_tensor(out=U32[:, a:b - 1], in0=AA[:, a + 1:b], scalar=lv(3), in1=U32[:, a:b - 1],
                                                             op0=ALU.mult, op1=ALU.add), reads=kall(iAA) + ["LV"], writes=kall(iU32))
            k.op("act", lambda e: e.activation(out=U16[:, 0:ntok], in_=U32[:, 0:ntok], func=AF.Copy), reads=kall(iU32), writes=kall(iU16))
            blk[n] = (s1, w1v, sg_, wgv)

        def steps3(n):
            nonlocal it
            (s1, w1v, sg_, wgv) = blk[n]

            def lv(i):
                return lvn(n, i)
            for dr in range(2):
                EA, iEA = (E1, iE1) if dr == 0 else (E3, iE3)
                pa_order = bwd_order if dr == 0 else fwd_order
                gsets = []
                for (ti, t0, tn, s) in pa_order:
                    gsets.append((it % 2, 2 + it % 2))
                    it += 1

                def stG(i):
                    (ti, t0, tn, s) = pa_order[i]
                    rb, ib = gsets[i]

                    def mmg(e):
                        e.matmul(PS[rb][0:P, :tn], lhsT=wgv[:, 2 * dr, :], rhs=U16[:, t0:t0 + tn], start=True, stop=True)
                        return e.matmul(PS[ib][0:P, :tn], lhsT=wgv[:, 2 * dr + 1, :], rhs=U16[:, t0:t0 + tn], start=True, stop=True)
                    k.op("pe", mmg, reads=[("W", sg_), kt(iU16, ti)], writes=[("PS", rb), ("PS", ib)])

                def stE(i):
                    (ti, t0, tn, s) = pa_order[i]
                    rb, ib = gsets[i]
                    ta, tak = (T1, "T1") if rb == 0 else (T3, "RS")
                    tb, tbk = (T2, "T2") if rb == 0 else (T4, "NM")
                    k.op("act", lambda e: e.activation(out=ta[:, :tn], in_=PS[rb][0:P, :tn], func=AF.Tanh, scale=0.5, bias=lv(11 + 2 * dr)),
                         reads=["LV"], writes=[("PS", rb), tak])
                    k.op("act", lambda e: e.activation(out=tb[:, :tn], in_=PS[ib][0:P, :tn], func=AF.Tanh, scale=0.5, bias=lv(12 + 2 * dr)),
                         reads=["LV"], writes=[("PS", ib), tbk])
                    k.op("act", lambda e: e.activation(out=AA[:, t0:t0 + tn], in_=ta[:, :tn], func=AF.Exp, scale=lv(15 + dr), bias=lv(15 + dr)),
                         reads=[tak, "LV"], writes=[kt(iAA, ti)])
                    if dr == 0:
                        k.op("dve", lambda e: e.tensor_tensor(out=EA[:, t0:t0 + tn], in0=AA[:, t0:t0 + tn], in1=AA[:, t0:t0 + tn], op=ALU.mult),
                             reads=[kt(iAA, ti)], writes=[kt(iEA, ti)])
                    else:
                        k.op("act", lambda e: e.activation(out=EA[:, t0:t0 + tn], in_=ta[:, :tn], func=AF.Exp, scale=lv(17 + dr), bias=lv(17 + dr)),
                             reads=[tak, "LV"], writes=[kt(iEA, ti)])
                    k.op("dve", lambda e: e.scalar_tensor_tensor(out=E2[:, t0:t0 + tn], in0=tb[:, :tn], scalar=1.0, in1=U32[:, t0:t0 + tn],
                                                                 op0=ALU.add, op1=ALU.mult), reads=[tbk, kt(iU32, ti)], writes=[kt(iE2, ti)])
                stG(0)
                for i in range(len(pa_order)):
                    if i + 1 < len(pa_order):
                        stG(i + 1)
                    stE(i)
                    self.bg_step(4)
                k.op("act", lambda e: e.activation(out=EA[:, 0:ntok], in_=EA[:, 0:ntok], func=AF.Sqrt, scale=-0.25, bias=self.CONST[0:P, 1:2]),
                     reads=["CONST"], writes=kall(iEA)[:NTI])
                sc_order = fwd_order if dr == 0 else bwd_order
                prev = None
                for (ti, t0, tn, s) in sc_order:
                    k.op("dve", lambda e: e.tensor_tensor(out=E2[:, t0:t0 + tn], in0=E2[:, t0:t0 + tn], in1=EA[:, t0:t0 + tn], op=ALU.mult),
                         reads=[kt(iEA, ti)], writes=[kt(iE2, ti)])
                    if prev is None:
                        init = 0.0
                        rk = []
                    else:
                        (pti, pt0, ptn, ps_) = prev
                        init = EA[:, pt0 + ptn - 1:pt0 + ptn] if dr == 0 else EA[:, pt0:pt0 + 1]
                        rk = [kt(iEA, pti)]
                    if dr == 0:
                        k.op("dve", lambda e: e.tensor_tensor_scan(out=EA[:, t0:t0 + tn], data0=AA[:, t0:t0 + tn], data1=E2[:, t0:t0 + tn],
                                                                   initial=init, op0=ALU.mult, op1=ALU.add),
                             reads=[kt(iAA, ti), kt(iE2, ti)] + rk, writes=[kt(iEA, ti)])
                    else:
                        k.op("dve", lambda e: e.tensor_tensor_scan(out=rev(EA, t0, t0 + tn), data0=rev(AA, t0, t0 + tn),
                                                                   data1=rev(E2, t0, t0 + tn), initial=init, op0=ALU.mult, op1=ALU.add),
                             reads=[kt(iAA, ti), kt(iE2, ti)] + rk, writes=[kt(iEA, ti)])
                    prev = (ti, t0, tn, s)

        def step4(n):
            nonlocal it
            (s1, w1v, sg_, wgv) = blk[n]
            self.bg_flush()
            yh = n % 2
            yo = yh * NT
            for (ti, t0, tn, s) in bwd_order:
                pb = it % 2
                it += 1
                k.op("dve", lambda e: e.tensor_tensor(out=E1[:, t0:t0 + tn], in0=E1[:, t0:t0 + tn], in1=E3[:, t0:t0 + tn], op=ALU.add),
                     reads=[kt(iE3, ti)], writes=[kt(iE1, ti)])

                def mm4(e):
                    for kc in range(8):
                        ins = e.matmul(PS[pb][0:P, :tn], lhsT=w1v[:, kc, 0, :], rhs=self.H[:, kc, t0:t0 + tn], start=(kc == 0), stop=(kc == 7))
                    return ins
                k.op("pe", mm4, reads=[("W", s1), ("H", ti)], writes=[("PS", pb)])
                tg, tgk = (T1, "T1") if pb == 0 else (T3, "RS")
                k.op("act", lambda e: e.activation(out=tg[:, :tn], in_=PS[pb][0:P, :tn], func=AF.Gelu_apprx_tanh), writes=[("PS", pb), tgk])
                k.op("dve", lambda e: e.tensor_tensor(out=Y16[0:P, yo + t0:yo + t0 + tn], in0=tg[:, :tn], in1=E1[:, t0:t0 + tn], op=ALU.mult),
                     reads=[tgk, kt(iE1, ti)], writes=[("LY", yh, ti)])
            if n % 2 == 1:
                sa, wa = self.wload(d["lruw2"][n - 1], D, parts=P)
                sb_, wb = self.wload(d["lruw2"][n], D, parts=P)

                def mkjob(dc, ti, t0, tn, s, sa=sa, wa=wa, sb_=sb_, wb=wb):
                    def run():
                        yb = 4 + self._lru_it % 2
                        self._lru_it += 1

                        def mm5(e):
                            e.matmul(PS[yb][:, :tn], lhsT=wa[:, dc * 128:(dc + 1) * 128], rhs=Y16[0:P, t0:t0 + tn], start=True, stop=False)
                            return e.matmul(PS[yb][:, :tn], lhsT=wb[:, dc * 128:(dc + 1) * 128], rhs=Y16[0:P, NT + t0:NT + t0 + tn],
                                            start=False, stop=True)
                        k.op("pe", mm5, reads=[("W", sa), ("W", sb_), ("LY", 0, ti), ("LY", 1, ti)], writes=[("PS", yb)])
                        self.x_update(l, 1, dc, ti, t0, tn, s, yb)
                    return run
                for dc in range(NCH):
                    for (ti, t0, tn, s) in tiles:
                        self.bg.append(mkjob(dc, ti, t0, tn, s))

        steps12(0)
        for n in range(NBLK):
            steps3(n)
            if n + 1 < NBLK:
                steps12(n + 1)
            step4(n)
        self.bg_flush()
        self.fence(guard + allL)

    def sc_mixer(self, l, tiles):
        k = self.k
        PS = self.PS
        d = self.d
        if not hasattr(self, "SCCW"):
            self.SCCW = self.st.enter_context(self.nc.sbuf_tensor("SCCW", [128, 3, NCH], F32))
        k.dma("sp", self.SCCW[:], d["sccw"], writes=["SCCW"])
        V = self.A[:, 8 * NT:10 * NT].bitcast(F32)
        C = self.SCR[:, 0:NT]
        guard = [("A", jj, ti) for jj in (8, 9, 10) for ti in range(5)]
        vk = [("SCV", ti) for ti in range(5)]
        self.fence(guard + vk)
        n = len(tiles)
        seq_of = lambda ti: (0, TL) if ti < 4 else (TL, NT)
        it = 0
        for c in range(NCH):
            sb_, wb = self.wload(d["scw1"][c, 0], 1024)
            sc_, wc = self.wload(d["scw1"][c, 1], 1024)
            su_, wu = self.wload(d["scw1"][c, 2], 1024)
            wb = wb.rearrange("p (kc n) -> p kc n", kc=8)
            wc = wc.rearrange("p (kc n) -> p kc n", kc=8)
            wu = wu.rearrange("p (kc n) -> p kc n", kc=8)
            w0 = self.SCCW[:, 0, c:c + 1]
            w1 = self.SCCW[:, 1, c:c + 1]
            w2 = self.SCCW[:, 2, c:c + 1]

            def stage1(i):
                nonlocal it
                (ti, t0, tn, s) = tiles[i]
                cb = it % 2
                ub = 2 + it % 2
                it += 1

                def mm(e):
                    for kc in range(8):
                        e.matmul(PS[cb][:, :tn], lhsT=wc[:, kc, :], rhs=self.H[:, kc, t0:t0 + tn], start=(kc == 0), stop=(kc == 7))
                    for kc in range(8):
                        ins = e.matmul(PS[ub][:, :tn], lhsT=wu[:, kc, :], rhs=self.H[:, kc, t0:t0 + tn], start=(kc == 0), stop=(kc == 7))
                    return ins
                k.op("pe", mm, reads=[("W", sc_), ("W", su_), ("H", ti)], writes=[("PS", cb), ("PS", ub)])
                cg = self.LNT[:, cb * 512:cb * 512 + tn]
                cgk = ["T1", "T2"][cb]
                k.op("act", lambda e: e.activation(out=cg, in_=PS[cb][:, :tn], func=AF.Copy), writes=[("PS", cb), cgk])
                k.op("dve", lambda e: e.tensor_tensor(out=V[:, t0:t0 + tn], in0=cg, in1=PS[ub][:, :tn], op=ALU.mult),
                     reads=[cgk], writes=[("PS", ub), ("SCV", ti)])

            def stage2(i):
                nonlocal it
                (ti, t0, tn, s) = tiles[i]
                (sa, se) = seq_of(ti)
                ck = ("SCR", ti)
                k.op("act", lambda e: e.activation(out=C[:, t0:t0 + tn], in_=V[:, t0:t0 + tn], func=AF.Copy, scale=w1),
                     reads=[("SCV", ti), "SCCW"], writes=[ck])
                a1 = max(t0, sa + 1)
                rk = [("SCV", ti)] + ([("SCV", ti - 1)] if t0 > sa else [])
                k.op("dve", lambda e: e.scalar_tensor_tensor(out=C[:, a1:t0 + tn], in0=V[:, a1 - 1:t0 + tn - 1], scalar=w0, in1=C[:, a1:t0 + tn],
                                                             op0=ALU.mult, op1=ALU.add), reads=rk + ["SCCW"], writes=[ck])
                b2 = min(t0 + tn, se - 1)
                rk = [("SCV", ti)] + ([("SCV", ti + 1)] if t0 + tn < se else [])
                k.op("dve", lambda e: e.scalar_tensor_tensor(out=C[:, t0:b2], in0=V[:, t0 + 1:b2 + 1], scalar=w2, in1=C[:, t0:b2],
                                                             op0=ALU.mult, op1=ALU.add), reads=rk + ["SCCW"], writes=[ck])
                bb = 4 + it % 2
                it += 1

                def mm2(e):
                    for kc in range(8):
                        ins = e.matmul(PS[bb][:, :tn], lhsT=wb[:, kc, :], rhs=self.H[:, kc, t0:t0 + tn], start=(kc == 0), stop=(kc == 7))
                    return ins
                k.op("pe", mm2, reads=[("W", sb_), ("H", ti)], writes=[("PS", bb)])
                k.op("dve", lambda e: e.tensor_tensor(out=self.Aview(c)[:, t0:t0 + tn], in0=PS[bb][:, :tn], in1=C[:, t0:t0 + tn], op=ALU.mult),
                     reads=[ck], writes=[("PS", bb), ("A", c, ti)])

            for i in range(n):
                stage1(i)
                if i >= 1:
                    stage2(i - 1)
            stage2(n - 1)
        self.fence(guard + vk)
        self.out_proj(l, d["scw2"], lambda kc: self.Aview(kc), lambda ti: [("A", kc, ti) for kc in range(8)], tiles)


def _prep_shared(inp):
    f = np.float32
    sh = {}
    mw = np.asarray(inp["mod_w"], f)
    sh["modw"] = np.ascontiguousarray(mw.reshape(DEPTH, 8, 128, 36, 256).transpose(0, 3, 2, 1, 4)).reshape(DEPTH, 36, 128, 2048)
    sh["modb"] = np.ascontiguousarray(np.asarray(inp["mod_b"], f).reshape(DEPTH, 72, 128).transpose(2, 0, 1))
    sh["lng"] = np.ascontiguousarray(np.asarray(inp["ln_g"], f).reshape(DEPTH, 3, NCH, 128).transpose(3, 0, 1, 2))
    sh["lnb"] = np.ascontiguousarray(np.asarray(inp["ln_b"], f).reshape(DEPTH, 3, NCH, 128).transpose(3, 0, 1, 2))
    w1 = np.asarray(inp["ffn_w_in"], f)
    g = w1[..., :DFF].reshape(DEPTH, 2, 8, 128, NJ, 128)
    u = w1[..., DFF:].reshape(DEPTH, 2, 8, 128, NJ, 128)
    gu = np.stack([g, u], axis=5)
    sh["ffw1"] = np.ascontiguousarray(gu.transpose(0, 1, 4, 3, 2, 5, 6)).reshape(DEPTH, 2, NJ, 128, 2048)
    w2 = np.asarray(inp["ffn_w_out"], f)
    w2 = w2.reshape(DEPTH, 2, 2, NJH, 128, NCH, 128)
    sh["ffw2"] = np.ascontiguousarray(w2.transpose(0, 1, 2, 5, 4, 3, 6)).reshape(DEPTH, 2, 2, NCH, 128, NJH * 128)
    sh["ident"] = np.eye(128, dtype=f)
    w = np.asarray(inp["sc_w_in"], f)[0].reshape(8, 128, 3, NCH, 128)
    sh["scw1"] = np.ascontiguousarray(w.transpose(3, 2, 1, 0, 4)).reshape(NCH, 3, 128, 1024)
    w = np.asarray(inp["sc_w_out"], f)[0].reshape(8, 128, NCH, 128)
    sh["scw2"] = np.ascontiguousarray(w.transpose(2, 1, 0, 3)).reshape(NCH, 128, 1024)
    sh["sccw"] = np.ascontiguousarray(np.asarray(inp["sc_conv_w"], f)[0].reshape(3, NCH, 128).transpose(2, 0, 1))
    w = np.asarray(inp["lru_w_in"], f)[0]
    wg = w[:, :DRNN].reshape(8, 128, NBLK, BW)
    wu = w[:, DRNN:].reshape(8, 128, NBLK, BW)
    sh["lruw1"] = np.ascontiguousarray(np.stack([wg, wu], axis=3).transpose(2, 1, 0, 3, 4)).reshape(NBLK, 128, 8 * 2 * BW)
    g = np.asarray(inp["lru_w_gates"], f)[0]
    sh["lrug"] = np.ascontiguousarray(g.transpose(2, 3, 0, 1, 4)).reshape(NBLK, BW, 4 * BW)
    sh["lruw2"] = np.ascontiguousarray(np.asarray(inp["lru_w_out"], f)[0].reshape(NBLK, BW, D))
    lv = np.concatenate([np.asarray(inp["lru_conv_w"], f)[0],
                         np.asarray(inp["lru_conv_b"], f)[0][None],
                         np.asarray(inp["lru_b_gates"], f)[0].reshape(4, DRNN),
                         np.asarray(inp["lru_lambda"], f)[0]], axis=0)
    sh["lruvec"] = np.ascontiguousarray(lv.reshape(11, NBLK, BW).transpose(2, 1, 0))
    w = np.asarray(inp["na_w_qkv"], f).reshape(2, 8, 128, 3, NCH, 128)
    sh["naw1"] = np.ascontiguousarray(w.transpose(0, 4, 3, 2, 1, 5)).reshape(2, NCH, 3, 128, 1024)
    w = np.asarray(inp["na_w_o"], f).reshape(2, 8, 128, NCH, 128)
    sh["naw2"] = np.ascontiguousarray(w.transpose(0, 3, 2, 1, 4)).reshape(2, NCH, 128, 1024)
    qc = np.arange(64)
    dcol = np.clip(qc[None, :] - qc[:, None], 1 - WIN_C, WIN_C - 1) + WIN_C - 1
    rpb = np.asarray(inp["na_rpb"], f)
    sh["rpbt"] = np.ascontiguousarray(rpb[:, :, :, dcol].transpose(0, 1, 3, 2, 4)).reshape(2, HEADS, 64, 15 * 64)
    cs = np.clip(qc - WIN_C // 2, 0, GRID_W - WIN_C)
    cm = ((qc[None, :] >= cs[:, None]) & (qc[None, :] < cs[:, None] + WIN_C)).astype(f)
    sh["cmask"] = np.ascontiguousarray(np.concatenate([cm, cm], axis=0))
    return sh


def _prep_core(inp, b):
    f = np.float32
    x = np.asarray(inp["x"], f)[b]
    ctx = np.asarray(inp["ctx"], f)[b]
    xin = np.ascontiguousarray(np.concatenate([x.T, ctx.T], axis=1))
    cc = np.stack([np.asarray(inp["c"], f)[b], np.asarray(inp["c_ctx"], f)], axis=1)
    cc = np.ascontiguousarray(cc.reshape(NCH, 128, 2).transpose(1, 0, 2))
    return {"xin": xin, "cc": cc}


_CACHE = {}


def kernel(**inputs):
    n = 8
    prog = _CACHE.get("prog")
    if prog is None:
        prog = Prog()
        prog.build()
        _CACHE["prog"] = prog
    sh = _prep_shared(inputs)
    in_maps = []
    for b in range(n):
        m = dict(sh)
        m.update(_prep_core(inputs, b))
        in_maps.append(m)
    res = run_bass_kernel_spmd(prog.nc, in_maps, core_ids=list(range(n)))
    out = np.empty((n, TL, D), dtype=np.float32)
    for b in range(n):
        out[b] = res.results[b]["out"][:, :TL].T
    return out
```
